# Optimizing a Trainium2 kernel written in Bass

```python
import math
import jax, jax.numpy as jnp
from jax import lax
import numpy as np

D_MODEL = 2048
BATCH = 8
SEQ = 4096
DEPTH = 4

N_MEM = 256
D_MIX = D_MODEL
GROUP_W = D_MIX // 4
GLA_HEADS = 4
GLA_DV = GROUP_W // GLA_HEADS
GLA_DK = GLA_DV // 2
GLA_QK = GLA_HEADS * GLA_DK
GLA_V = GLA_HEADS * GLA_DV
GLA_RANK = 16
GLA_GATE_NORM = 16.0
GLA_CHUNK = 64
FNET_GROUPS = 4
FNET_CH = GROUP_W // FNET_GROUPS
HY_W = GROUP_W
HY_ORDER = 2
HY_BANDS = 16
HY_EMB = 2 * HY_BANDS + 1
HY_FFN = 64
HY_DECAY_SLOW = -math.log(1e-2) / 1.5
HY_DECAY_FAST = -math.log(1e-2) / 0.3
SC_W = GROUP_W
SHORT_W = 3
IN_SPLITS = (GLA_QK, GLA_QK, GLA_V, GLA_V, 2 * GLA_RANK, GROUP_W, 3 * HY_W, 3 * SC_W)
D_IN = 2 * GLA_QK + 2 * GLA_V + 2 * GLA_RANK + GROUP_W + 3 * HY_W + 3 * SC_W
XA_HEADS = 4
XA_HD = D_MODEL // XA_HEADS
D_FF = ((8 * D_MODEL // 3 + 255) // 256) * 256
EPS = 1e-6

kernel_name = 'hybrid_parallel_group_encoder'


def rms_norm(x, g):
    xf = x.astype(jnp.float32)
    y = xf * lax.rsqrt(jnp.mean(xf * xf, axis=-1, keepdims=True) + EPS)
    return (y * g.astype(jnp.float32)).astype(x.dtype)


def short_conv(u, w):
    up = jnp.pad(u, ((0, 0), (1, 1), (0, 0)))
    return up[:, :-2] * w[0] + up[:, 1:-1] * w[1] + up[:, 2:] * w[2]


def gla_direction(q, k, v, gk, strict):
    bsz, seq, heads, dk = q.shape
    dv = v.shape[-1]
    n = seq // GLA_CHUNK
    rs = lambda t: t.reshape(bsz, n, GLA_CHUNK, heads, t.shape[-1])
    q, k, v, gk = rs(q), rs(k), rs(v), rs(gk)
    b = jnp.cumsum(gk, axis=2)
    b_last = b[:, :, -1]
    b_ref = b[:, :, GLA_CHUNK // 2:GLA_CHUNK // 2 + 1]
    scores = jnp.einsum('bnihk,bnjhk->bnhij', q * jnp.exp(b - b_ref), k * jnp.exp(b_ref - b))
    mask = jnp.tril(jnp.ones((GLA_CHUNK, GLA_CHUNK), dtype=bool), k=-1 if strict else 0)
    scores = jnp.where(mask, scores, 0.0)
    o_intra = jnp.einsum('bnhij,bnjhv->bnihv', scores, v)
    u = jnp.einsum('bnchk,bnchv->bnhkv', k * jnp.exp(b_last[:, :, None] - b), v)
    decay = jnp.exp(b_last)

    def step(state, inp):
        d, un = inp
        return state * d[..., None] + un, state

    init = jnp.zeros((bsz, heads, dk, dv), dtype=jnp.float32)
    _, states = lax.scan(step, init, (jnp.moveaxis(decay, 1, 0), jnp.moveaxis(u, 1, 0)))
    states = jnp.moveaxis(states, 0, 1)
    o_inter = jnp.einsum('bnchk,bnhkv->bnchv', q * jnp.exp(b), states)
    return (o_intra + o_inter).reshape(bsz, seq, heads, dv)


def gla_mixer(q, k, v, g, lr, gk_w, gk_b, norm_g):
    f32 = jnp.float32
    bsz, seq, _ = q.shape
    heads = lambda t, d: t.astype(f32).reshape(bsz, seq, GLA_HEADS, d)
    q = heads(q, GLA_DK) * GLA_DK ** -0.5
    k = heads(k, GLA_DK)
    v = heads(v, GLA_DV)
    lr = lr.astype(f32)
    gk_fwd = jax.nn.log_sigmoid(lr[..., :GLA_RANK] @ gk_w[0].astype(f32) + gk_b[0].astype(f32)) / GLA_GATE_NORM
    gk_bwd = jax.nn.log_sigmoid(lr[..., GLA_RANK:] @ gk_w[1].astype(f32) + gk_b[1].astype(f32)) / GLA_GATE_NORM
    gk_fwd = gk_fwd.reshape(bsz, seq, GLA_HEADS, GLA_DK)
    gk_bwd = gk_bwd.reshape(bsz, seq, GLA_HEADS, GLA_DK)
    flip = lambda t: jnp.flip(t, axis=1)
    o = gla_direction(q, k, v, gk_fwd, False) + flip(gla_direction(flip(q), flip(k), flip(v), flip(gk_bwd), True))
    o = rms_norm(o, norm_g) * jax.nn.silu(heads(g, GLA_DV))
    return o.reshape(bsz, seq, GLA_V)


def fnet_mixer(u):
    bsz, seq, _ = u.shape
    uf = u.astype(jnp.float32).reshape(bsz, seq, FNET_GROUPS, FNET_CH)
    return jnp.fft.fftn(uf, axes=(1, 3), norm='ortho').real.reshape(bsz, seq, GROUP_W)


def hyena_position_features(seq):
    pos = jnp.arange(seq, dtype=jnp.float32)
    t = pos / seq
    f = jnp.linspace(1e-4, HY_BANDS - 1, HY_BANDS, dtype=jnp.float32)
    ang = (2.0 * math.pi * t)[:, None] * f[None, :]
    return jnp.concatenate([t[:, None], jnp.cos(ang), -jnp.sin(ang)], axis=-1)


def hyena_filters(feats, w1, b1, w2, b2, w3, freq, decay):
    f32 = jnp.float32
    seq = feats.shape[0]
    freq = freq.astype(f32)
    h = jnp.sin(freq * (feats @ w1.astype(f32) + b1.astype(f32)))
    h = jnp.sin(freq * (h @ w2.astype(f32) + b2.astype(f32)))
    h = h @ w3.astype(f32)
    window = jnp.exp(-feats[:, :1] * jnp.abs(decay.astype(f32).reshape(-1)))
    h = (h * window).reshape(seq, HY_ORDER, 2, HY_W)
    h_fwd, h_bwd = h[:, :, 0], h[:, :, 1]
    two_sided = jnp.concatenate([h_fwd, jnp.zeros_like(h_fwd[:1]), h_bwd[:0:-1]], axis=0)
    return jnp.moveaxis(jnp.fft.rfft(two_sided, axis=0), 1, 0)


def long_conv(z, filt, skip):
    seq = z.shape[1]
    zf = jnp.fft.rfft(z, n=2 * seq, axis=1)
    y = jnp.fft.irfft(zf * filt[None], n=2 * seq, axis=1)[:, :seq]
    return y + z * skip.astype(jnp.float32)


def hyena_mixer(u, conv_w, filt, skip):
    u = short_conv(u, conv_w).astype(jnp.float32)
    v, x1, x2 = jnp.split(u, 3, axis=-1)
    z = x1 * long_conv(v, filt[0], skip[0])
    return x2 * long_conv(z, filt[1], skip[1])


def shortconv_mixer(u, conv_w):
    b, c, h = jnp.split(u, 3, axis=-1)
    return b * short_conv(c * h, conv_w)


def cross_attention(h, mem_n, w_q, w_kv, w_o):
    bsz, seq, _ = h.shape
    n_mem = mem_n.shape[1]
    q = (h @ w_q).reshape(bsz, seq, XA_HEADS, XA_HD)
    k, v = jnp.split(mem_n @ w_kv, 2, axis=-1)
    k = k.reshape(bsz, n_mem, XA_HEADS, XA_HD)
    v = v.reshape(bsz, n_mem, XA_HEADS, XA_HD)
    s = jnp.einsum('bshd,bmhd->bhsm', q, k).astype(jnp.float32) * XA_HD ** -0.5
    p = jax.nn.softmax(s, axis=-1).astype(v.dtype)
    o = jnp.einsum('bhsm,bmhd->bshd', p, v).reshape(bsz, seq, D_MODEL)
    return o @ w_o


def swiglu(h, w_gate_up, w_down):
    gate, up = jnp.split(h @ w_gate_up, 2, axis=-1)
    return (jax.nn.silu(gate) * up) @ w_down


def setup_inputs(seed: int = 0) -> dict:
    key = jax.random.key(seed)
    ks = jax.random.split(key, 32)
    nrm = lambda k, shape, scale: jax.random.normal(k, shape, dtype=jnp.float32) * scale
    gain = lambda k, shape: 1.0 + 0.02 * jax.random.normal(k, shape, dtype=jnp.float32)
    decay_base = jnp.linspace(HY_DECAY_SLOW, HY_DECAY_FAST, HY_W, dtype=jnp.float32)
    return {
        'x': nrm(ks[0], (BATCH, SEQ, D_MODEL), 1.0),
        'mem': nrm(ks[1], (BATCH, N_MEM, D_MODEL), 1.0),
        'norm_g': gain(ks[2], (DEPTH, 3, D_MODEL)),
        'w_in': nrm(ks[3], (DEPTH, D_MODEL, D_IN), D_MODEL ** -0.5),
        'gla_gk_w': nrm(ks[4], (DEPTH, 2, GLA_RANK, GLA_QK), GLA_RANK ** -0.5),
        'gla_gk_b': nrm(ks[5], (DEPTH, 2, GLA_QK), 0.1),
        'gla_norm_g': gain(ks[6], (DEPTH, GLA_DV)),
        'hy_conv_w': nrm(ks[7], (DEPTH, SHORT_W, 3 * HY_W), SHORT_W ** -0.5),
        'hy_ffn_w1': nrm(ks[8], (DEPTH, HY_EMB, HY_FFN), HY_EMB ** -0.5),
        'hy_ffn_b1': nrm(ks[9], (DEPTH, HY_FFN), 0.02),
        'hy_ffn_w2': nrm(ks[10], (DEPTH, HY_FFN, HY_FFN), HY_FFN ** -0.5),
        'hy_ffn_b2': nrm(ks[11], (DEPTH, HY_FFN), 0.02),
        'hy_ffn_w3': nrm(ks[12], (DEPTH, HY_FFN, HY_ORDER * 2 * HY_W), 0.1 * HY_FFN ** -0.5),
        'hy_sin_freq': gain(ks[13], (DEPTH, HY_FFN)),
        'hy_decay': decay_base * gain(ks[14], (DEPTH, HY_ORDER, 2, HY_W)),
        'hy_skip': nrm(ks[15], (DEPTH, HY_ORDER, HY_W), 1.0),
        'sc_conv_w': nrm(ks[16], (DEPTH, SHORT_W, SC_W), SHORT_W ** -0.5),
        'grp_norm_g': gain(ks[17], (DEPTH, 3, GROUP_W)),
        'w_out': nrm(ks[18], (DEPTH, D_MIX, D_MODEL), D_MIX ** -0.5),
        'mem_norm_g': gain(ks[19], (D_MODEL,)),
        'w_xq': nrm(ks[20], (DEPTH, D_MODEL, D_MODEL), D_MODEL ** -0.5),
        'w_xkv': nrm(ks[21], (DEPTH, D_MODEL, 2 * D_MODEL), D_MODEL ** -0.5),
        'w_xo': nrm(ks[22], (DEPTH, D_MODEL, D_MODEL), D_MODEL ** -0.5),
        'w_gate_up': nrm(ks[23], (DEPTH, D_MODEL, 2 * D_FF), D_MODEL ** -0.5),
        'w_down': nrm(ks[24], (DEPTH, D_FF, D_MODEL), D_FF ** -0.5),
        'final_norm_g': gain(ks[25], (D_MODEL,)),
    }


def reference(x, mem, norm_g, w_in, gla_gk_w, gla_gk_b, gla_norm_g, hy_conv_w, hy_ffn_w1, hy_ffn_b1,
              hy_ffn_w2, hy_ffn_b2, hy_ffn_w3, hy_sin_freq, hy_decay, hy_skip, sc_conv_w, grp_norm_g,
              w_out, mem_norm_g, w_xq, w_xkv, w_xo, w_gate_up, w_down, final_norm_g):
    feats = hyena_position_features(x.shape[1])
    mem_n = rms_norm(mem, mem_norm_g)
    split_at = np.cumsum(IN_SPLITS)[:-1].tolist()
    for l in range(DEPTH):
        h = rms_norm(x, norm_g[l, 0])
        q, k, v, g, lr, u_f, u_h, u_s = jnp.split(h @ w_in[l], split_at, axis=-1)
        y_a = gla_mixer(q, k, v, g, lr, gla_gk_w[l], gla_gk_b[l], gla_norm_g[l])
        y_b = rms_norm(fnet_mixer(u_f), grp_norm_g[l, 0])
        filt = hyena_filters(feats, hy_ffn_w1[l], hy_ffn_b1[l], hy_ffn_w2[l], hy_ffn_b2[l],
                             hy_ffn_w3[l], hy_sin_freq[l], hy_decay[l])
        y_c = rms_norm(hyena_mixer(u_h, hy_conv_w[l], filt, hy_skip[l]), grp_norm_g[l, 1])
        y_d = rms_norm(shortconv_mixer(u_s, sc_conv_w[l]), grp_norm_g[l, 2])
        mix = jnp.concatenate([y_a.astype(x.dtype), y_b.astype(x.dtype), y_c.astype(x.dtype), y_d.astype(x.dtype)], axis=-1)
        x = x + mix @ w_out[l]
        x = x + cross_attention(rms_norm(x, norm_g[l, 1]), mem_n, w_xq[l], w_xkv[l], w_xo[l])
        x = x + swiglu(rms_norm(x, norm_g[l, 2]), w_gate_up[l], w_down[l])
    return rms_norm(x, final_norm_g)
```

```python
import math
import numpy as np
import ml_dtypes
import concourse.bass as bass
import concourse.mybir as mybir
from concourse.bass_utils import run_bass_kernel_spmd

F32 = mybir.dt.float32
BF16 = mybir.dt.bfloat16
AF = mybir.ActivationFunctionType
ALU = mybir.AluOpType

D = 2048
L = 4096
DEPTH = 4
NMEM = 256
D_IN = 5152
D_FF = 5632
EPS = 1e-6
KC = D // 128

ENGS = ("sync", "scalar", "vector", "gpsimd", "tensor")
DMA_K = 8


class Buf:
    __slots__ = ("name", "w", "rs", "rd", "wc", "wd")

    def __init__(self, name=""):
        self.name = name
        self.w = None
        self.wc = {}
        self.wd = []
        self.rs = {}
        self.rd = []


class Op:
    __slots__ = ("eng", "fn", "deps", "sig", "sem", "tick", "is_dma")

    def __init__(self, eng, fn, is_dma=False):
        self.eng = eng
        self.fn = fn
        self.deps = set()
        self.sig = False
        self.sem = None
        self.tick = 0
        self.is_dma = is_dma


class Prog:
    def __init__(self, nc):
        self.nc = nc
        self.ops = {e: [] for e in ENGS}
        self.esem = {e: nc.alloc_semaphore("se_" + e) for e in ("scalar", "vector", "gpsimd", "tensor")}
        self.dsem = {}
        self.dn = {}
        self.dlast = {}
        self.dcnt = {}
        self.stores = []
        self.last_real = {}
        self.live_dma = []

    def _track(self, o, reads, writes, mwrites=()):
        deps = o.deps
        for b in reads:
            if b.w is not None:
                deps.add(b.w)
            deps.update(b.wc.values())
            deps.update(b.wd)
        for b in writes:
            if b.w is not None:
                deps.add(b.w)
            deps.update(b.wc.values())
            deps.update(b.wd)
            deps.update(b.rs.values())
            deps.update(b.rd)
        for b in mwrites:
            if b.rs or b.rd:
                deps.update(b.rs.values())
                deps.update(b.rd)
                b.rs = {}
                b.rd = []
                b.wc = {}
                b.wd = []
                b.w = None
            if b.w is not None:
                deps.add(b.w)
        for b in reads:
            if o.is_dma:
                b.rd.append(o)
            else:
                b.rs[o.eng] = o
        for b in writes:
            b.w = o
            b.wc = {}
            b.wd = []
            b.rs = {}
            b.rd = []
        for b in mwrites:
            if o.is_dma:
                b.wd.append(o)
            else:
                b.wc[o.eng] = o
        deps.discard(o)

    def op(self, eng, fn, reads=(), writes=(), mwrites=(), extra=()):
        o = Op(eng, fn)
        o.sem = self.esem[eng]
        self._track(o, reads, writes, mwrites)
        o.deps.update(extra)
        self.ops[eng].append(o)
        self.last_real[eng] = o
        return o

    def dma(self, q, out, in_, reads=(), writes=(), mwrites=(), store=False):
        def fn(eng, out=out, in_=in_):
            return eng.dma_start(out=out, in_=in_)
        o = Op(q, fn, is_dma=True)
        self._track(o, reads, writes, mwrites)
        if q not in self.dsem:
            self.dsem[q] = [self.nc.alloc_semaphore("sd_%s%d" % (q, i)) for i in range(DMA_K)]
            self.dn[q] = 0
            self.dlast[q] = [None] * DMA_K
            self.dcnt[q] = [0] * DMA_K
        slot = self.dn[q] % DMA_K
        self.dn[q] += 1
        prev = self.dlast[q][slot]
        if prev is not None:
            o.deps.add(prev)
        self.dlast[q][slot] = o
        self.dcnt[q][slot] += 1
        o.sem = self.dsem[q][slot]
        o.tick = 16 * self.dcnt[q][slot]
        o.sig = True
        self.ops[q].append(o)
        if store:
            self.stores.append(o)
        return o

    def barrier(self):
        deps = set(self.last_real.values())
        for q in self.dlast:
            for p in self.dlast[q]:
                if p is not None:
                    deps.add(p)
        for e in ENGS:
            if not self.ops[e]:
                continue
            o = Op(e, lambda eng: None)
            o.deps.update(deps)
            self.ops[e].append(o)

    def finish(self):
        o = Op("sync", lambda eng: None)
        o.deps.update(self.stores)
        for q in self.dlast:
            for p in self.dlast[q]:
                if p is not None:
                    o.deps.add(p)
        self.ops["sync"].append(o)
        for e in ENGS:
            for o in self.ops[e]:
                for d in o.deps:
                    d.sig = True
        for e in ("scalar", "vector", "gpsimd", "tensor"):
            c = 0
            for o in self.ops[e]:
                if o.is_dma:
                    continue
                if o.sig:
                    c += 1
                    o.tick = c
        nc = self.nc
        with nc.Block() as block:
            for e in ENGS:
                if not self.ops[e]:
                    continue

                def body(eng, e=e):
                    waited = {}
                    for o in self.ops[e]:
                        need = {}
                        for d in o.deps:
                            k = d.sem.num
                            if need.get(k, (None, 0))[1] < d.tick:
                                need[k] = (d.sem, d.tick)
                        for k, (sem, v) in need.items():
                            if waited.get(k, 0) < v:
                                eng.wait_ge(sem, v)
                                waited[k] = v
                        ins = o.fn(eng)
                        if ins is not None and o.sig:
                            ins.then_inc(o.sem, 16 if o.is_dma else 1)

                getattr(block, e)(body)


_UID = [0]


def sbt(nc, name, shape, dt):
    _UID[0] += 1
    return nc.sbuf_tensor("%s_%d" % (name, _UID[0]), shape, dt)


class Pool:
    def __init__(self, items):
        self.items = items
        self.i = 0

    def next(self):
        it = self.items[self.i % len(self.items)]
        self.i += 1
        return it


WELEMS = 5760
NW = 4
NST = 4


class Ctx:
    def __init__(self, nc):
        self.nc = nc
        self.P = Prog(nc)
        self.psum = Pool([(nc.alloc_psum_tensor("ps%d" % i, [128, 512], F32), Buf("ps%d" % i)) for i in range(8)])
        self.wpool = Pool([(nc.alloc_sbuf_tensor("wb%d" % i, [128, WELEMS], BF16), Buf("wb%d" % i)) for i in range(NW)])
        self.stage = Pool([(nc.alloc_sbuf_tensor("stg%d" % i, [128, 512], F32), Buf("stg%d" % i)) for i in range(NST)])
        self.ones_b = nc.alloc_sbuf_tensor("ones_b", [128, 128], BF16)
        self.b_const = Buf("const")
        self.P.op("vector", lambda e: e.memset(self.ones_b[:], 1.0), mwrites=[self.b_const])
        self.eps_col = nc.alloc_sbuf_tensor("eps_col", [128, 1], F32)
        self.P.op("vector", lambda e: e.memset(self.eps_col[:], EPS), mwrites=[self.b_const])
        self.rpool = Pool([(nc.alloc_sbuf_tensor("rtile%d" % i, [128, 512], F32), Buf("rt%d" % i)) for i in range(4)])
        self.evac_i = 0

    def evac_eng(self):
        self.evac_i += 1
        return "vector" if self.evac_i % 2 else "scalar"


def copy_op(c, eng, out, in_, reads, writes):
    if eng == "scalar":
        return c.P.op("scalar", lambda e: e.copy(out=out, in_=in_), reads=reads, writes=writes)
    return c.P.op(eng, lambda e: e.tensor_copy(out=out, in_=in_), reads=reads, writes=writes)


def load_wtile(c, Wv, kcn, c0, mw):
    wt, bw = c.wpool.next()
    view = wt[:, :kcn * mw].rearrange("p (k m) -> p k m", m=mw)
    c.P.dma("gpsimd", view, Wv[:, :, c0:c0 + mw], writes=[bw])
    return view, bw


def linear_fm(c, act, b_act, kcn, TB, W, cols, epi, mw_tile=None, pre=None, tw=512):
    P = c.P
    Wv = W.rearrange("(k p) m -> p k m", p=128)
    c0, n = cols
    if mw_tile is None:
        mw_tile = 256 if kcn <= 22 else 128
    tw = min(tw, TB)
    steps = []
    off = 0
    while off < n:
        mw = min(mw_tile, n - off)
        first = True
        for mj in range(0, mw, 128):
            w = min(128, mw - mj)
            for tt in range(TB // tw):
                steps.append((off, mw, mj, w, tt, first))
                first = False
        off += mw
    toks = {}
    if pre is not None and steps:
        o, mw, mj, w, tt, f = steps[0]
        toks[0] = pre(o + mj, w, tt)
    wv = bw = None
    for i, (o, mw, mj, w, tt, f) in enumerate(steps):
        if f:
            wv, bw = load_wtile(c, Wv, kcn, c0 + o, mw)
        if pre is not None and i + 1 < len(steps):
            o2, mw2, mj2, w2, tt2, f2 = steps[i + 1]
            toks[i + 1] = pre(o2 + mj2, w2, tt2)
        ps, bps = c.psum.next()

        def mm(e, wv=wv, mj=mj, w=w, tt=tt, ps=ps):
            for k in range(kcn):
                ins = e.matmul(ps[:w, :tw], wv[:, k, mj:mj + w], act[:, k, tt * tw:(tt + 1) * tw],
                               start=(k == 0), stop=(k == kcn - 1))
            return ins
        P.op("tensor", mm, reads=[bw, b_act], writes=[bps])
        epi(o + mj, w, tt, ps[:w, :tw], bps, toks.pop(i, None))


def linear_tm(c, act, b_act, kcn, TB, W, cols, epi):
    P = c.P
    Wv = W.rearrange("(k p) m -> p k m", p=128)
    c0, n = cols
    off = 0
    while off < n:
        mw = min(256, n - off)
        wv, bw = load_wtile(c, Wv, kcn, c0 + off, mw)
        for tc in range(TB // 128):
            ps, bps = c.psum.next()

            def mm(e, wv=wv, tc=tc, ps=ps, mw=mw):
                for k in range(kcn):
                    ins = e.matmul(ps[:, :mw], act[:, k, tc * 128:(tc + 1) * 128], wv[:, k, :],
                                   start=(k == 0), stop=(k == kcn - 1))
                return ins
            P.op("tensor", mm, reads=[bw, b_act], writes=[bps])
            epi(off, mw, tc, ps[:, :mw], bps)
        off += mw


def rmsnorm_fm(c, src, b_src, tok0, TB, gcols, hn, b_hn, NT=128, out_dram=None):
    nc, P = c.nc, c.P
    srcv = src.rearrange("(k p) t -> p k t", p=128)
    with sbt(nc, "nx0", [128, KC, NT], F32) as x0, sbt(nc, "nx1", [128, KC, NT], F32) as x1, \
            sbt(nc, "nsq0", [128, KC, NT], BF16) as s0, sbt(nc, "nsq1", [128, KC, NT], BF16) as s1, \
            sbt(nc, "nrs0", [128, NT], F32) as r0, sbt(nc, "nrs1", [128, NT], F32) as r1:
        xs = Pool([(x0, Buf()), (x1, Buf())])
        ss = Pool([(s0, Buf()), (s1, Buf())])
        rs = Pool([(r0, Buf()), (r1, Buf())])
        for i in range(TB // NT):
            xt, bx = xs.next()
            sq, bs = ss.next()
            rt, br = rs.next()
            t0 = tok0 + i * NT
            P.dma("sync", xt[:], srcv[:, :, t0:t0 + NT], reads=[b_src], writes=[bx])
            P.op("scalar", lambda e, sq=sq, xt=xt: e.activation(out=sq[:], in_=xt[:], func=AF.Square),
                 reads=[bx], writes=[bs])
            ps, bps = c.psum.next()

            def mm(e, sq=sq, ps=ps):
                for k in range(KC):
                    ins = e.matmul(ps[:, :NT], c.ones_b[:], sq[:, k, :], start=(k == 0), stop=(k == KC - 1))
                return ins
            P.op("tensor", mm, reads=[bs, c.b_const], writes=[bps])
            P.op("scalar", lambda e, rt=rt, ps=ps: e.activation(out=rt[:], in_=ps[:, :NT], func=AF.Sqrt, bias=c.eps_col[:, 0:1],
                                                                scale=1.0 / D), reads=[bps, c.b_const], writes=[br])
            P.op("vector", lambda e, rt=rt: e.reciprocal(out=rt[:], in_=rt[:]), reads=[br], writes=[br])
            if out_dram is not None:
                dst, bdst = out_dram
                dstv = dst.rearrange("(k p) t -> p k t", p=128)
                for k in range(KC):
                    P.op("vector", lambda e, k=k, xt=xt, rt=rt: e.scalar_tensor_tensor(
                        out=xt[:, k, :], in0=xt[:, k, :], scalar=gcols[:, k:k + 1], in1=rt[:],
                        op0=ALU.mult, op1=ALU.mult), reads=[bx, br, c.b_const], writes=[bx])
                P.dma("sync", dstv[:, :, t0:t0 + NT], xt[:], reads=[bx], mwrites=[bdst], store=True)
                continue
            for k in range(KC):
                P.op("vector", lambda e, k=k, xt=xt, rt=rt, i=i: e.scalar_tensor_tensor(
                    out=hn[:, k, i * NT:(i + 1) * NT], in0=xt[:, k, :], scalar=gcols[:, k:k + 1], in1=rt[:],
                    op0=ALU.mult, op1=ALU.mult), reads=[bx, br, c.b_const], mwrites=[b_hn])
    P.barrier()


NPP = 120
PP_NG = 0
PP_GLAG = 48
PP_HCW = 49
PP_HSKIP = 85
PP_SCW = 93
PP_GRPG = 105
PP_HB1 = 117
PP_HB2 = 118
PP_HFREQ = 119

SEGS = [
    ("qT", 0, 256, "fm32"), ("kT", 256, 256, "fm32"), ("vtok", 512, 512, "tm16"), ("gT", 1024, 512, "fm32"),
    ("lrT", 1536, 32, "fm32"), ("ufT", 1568, 512, "fm16"), ("uhT", 2080, 1536, "fm32"), ("usT", 3616, 1536, "fm32"),
]

WEIGHTS = [("w_in", [DEPTH, D, D_IN]), ("w_out", [DEPTH, D, D]), ("w_xq", [DEPTH, D, D]), ("w_xkv", [DEPTH, D, 2 * D]),
           ("w_xo", [DEPTH, D, D]), ("w_gate_up", [DEPTH, D, 2 * D_FF]), ("w_down", [DEPTH, D_FF, D])]
SMALL = [("pp", [DEPTH, 128, NPP]), ("pg", [128, 32]), ("gkw", [DEPTH, 2, 32, 256]), ("gkb", [DEPTH, 2, 256]),
         ("hw1", [DEPTH, 33, 64]), ("hw2", [DEPTH, 64, 64]), ("hw3", [DEPTH, 64, 2048]), ("hdec", [DEPTH, 2048])]
SCRATCH = [("xres", [D, L], F32), ("qT", [256, L], F32), ("kT", [256, L], F32), ("vtok", [L, 512], BF16),
           ("gT", [512, L], F32), ("lrT", [32, L], F32), ("ufT", [512, L], BF16), ("uhT", [1536, L], F32),
           ("usT", [1536, L], F32), ("mixT", [D, L], BF16)]


def declare(nc, dbg_out=(), dbg_in=()):
    T = {}
    B = {}
    T["xT"] = nc.dram_tensor("xT", [D, L], F32, kind="ExternalInput").ap()
    T["memT"] = nc.dram_tensor("memT", [D, NMEM], F32, kind="ExternalInput").ap()
    for n, shp in WEIGHTS + SMALL:
        T[n] = nc.dram_tensor(n, shp, F32, kind="ExternalInput").ap()
    for n, shp, dt in SCRATCH:
        kind = "ExternalOutput" if n in dbg_out else ("ExternalInput" if n in dbg_in else "Internal")
        T[n] = nc.dram_tensor(n, shp, dt, kind=kind).ap()
    for n in T:
        B[n] = Buf(n)
    return T, B


def load_params(c, T, B):
    nc, P = c.nc, c.P
    c.pp = nc.alloc_sbuf_tensor("pp_sb", [128, DEPTH, NPP], F32)
    c.pg = nc.alloc_sbuf_tensor("pg_sb", [128, 32], F32)
    for l in range(DEPTH):
        P.dma("sync", c.pp[:, l, :], T["pp"][l], mwrites=[c.b_const])
    P.dma("sync", c.pg[:], T["pg"], mwrites=[c.b_const])


TB = 1024


def phase_inproj(c, T, B, l, xname):
    nc, P = c.nc, c.P
    W = T["w_in"][l]
    for blk in range(L // TB):
        tok0 = blk * TB
        with sbt(nc, "hn", [128, KC, TB], BF16) as hn:
            b_hn = Buf("hn")
            rmsnorm_fm(c, T[xname], B[xname], tok0, TB, c.pp[:, l, PP_NG:PP_NG + 16], hn, b_hn)
            for name, c0, n, kind in SEGS:
                dst, bd = T[name], B[name]
                if kind == "tm16":
                    def epi(off, mw, tc, ps, bps, dst=dst, bd=bd):
                        st, bst = c.stage.next()
                        sv = st[:].bitcast(BF16)[:, :mw]
                        copy_op(c, c.evac_eng(), sv, ps, [bps], [bst])
                        r0 = tok0 + tc * 128
                        P.dma("sync", dst[r0:r0 + 128, off:off + mw], sv, reads=[bst], mwrites=[bd])
                    linear_tm(c, hn, b_hn, KC, TB, W, (c0, n), epi)
                else:
                    def epi(ci, w, tt, ps, bps, tok, dst=dst, bd=bd, kind=kind):
                        st, bst = c.stage.next()
                        sv = st[:w, :] if kind == "fm32" else st[:].bitcast(BF16)[:w, :512]
                        copy_op(c, c.evac_eng(), sv, ps, [bps], [bst])
                        t0 = tok0 + tt * 512
                        P.dma("sync", dst[ci:ci + w, t0:t0 + 512], sv, reads=[bst], mwrites=[bd])
                    linear_fm(c, hn, b_hn, KC, TB, W, (c0, n), epi)
            P.barrier()


def make_resid(c, T, B, xin, xout, tok0):
    P, nc = c.P, c.nc

    def pre(ci, w, tt):
        xt, bx = c.rpool.next()
        t0 = tok0 + tt * 512
        P.dma("sync", xt[:w, :], T[xin][ci:ci + w, t0:t0 + 512], reads=[B[xin]], writes=[bx])
        return xt, bx

    def epi(ci, w, tt, ps, bps, tok):
        xt, bx = tok
        t0 = tok0 + tt * 512
        P.op("vector", lambda e: e.tensor_tensor(out=xt[:w, :], in0=ps, in1=xt[:w, :], op=ALU.add), reads=[bps, bx], writes=[bx])
        P.dma("sync", T[xout][ci:ci + w, t0:t0 + 512], xt[:w, :], reads=[bx], mwrites=[B[xout]])
    return pre, epi


def phase_outproj(c, T, B, l, xin, xout):
    nc, P = c.nc, c.P
    mv = T["mixT"].rearrange("(k p) t -> p k t", p=128)
    for blk in range(L // TB):
        tok0 = blk * TB
        with sbt(nc, "mix", [128, KC, TB], BF16) as mix:
            b_mix = Buf("mix")
            for k in range(0, KC, 4):
                P.dma("sync", mix[:, k:k + 4, :], mv[:, k:k + 4, tok0:tok0 + TB], reads=[B["mixT"]], mwrites=[b_mix])
            pre, epi = make_resid(c, T, B, xin, xout, tok0)
            linear_fm(c, mix, b_mix, KC, TB, T["w_out"][l], (0, D), epi, pre=pre)
            P.barrier()


def phase_ffn(c, T, B, l, xin, xout):
    nc, P = c.nc, c.P
    FC = D_FF // 128
    Wgu = T["w_gate_up"][l]
    Wv = Wgu.rearrange("(k p) m -> p k m", p=128)
    for blk in range(L // TB):
        tok0 = blk * TB
        with sbt(nc, "h3", [128, KC, TB], BF16) as h3:
            b_h3 = Buf("h3")
            rmsnorm_fm(c, T[xin], B[xin], tok0, TB, c.pp[:, l, PP_NG + 32:PP_NG + 48], h3, b_h3)
            with sbt(nc, "aT", [128, FC, TB], BF16) as aT:
                b_aT = Buf("aT")
                for j0 in range(0, FC, 2):
                    wg, bwg = load_wtile(c, Wv, KC, j0 * 128, 256)
                    wu, bwu = load_wtile(c, Wv, KC, D_FF + j0 * 128, 256)
                    for mj in range(2):
                        j = j0 + mj
                        for tt in range(TB // 512):
                            psg, bpg = c.psum.next()
                            psu, bpu = c.psum.next()

                            def mm(e, wt, ps, mj=mj, tt=tt):
                                for k in range(KC):
                                    ins = e.matmul(ps[:, :], wt[:, k, mj * 128:(mj + 1) * 128], h3[:, k, tt * 512:(tt + 1) * 512],
                                                   start=(k == 0), stop=(k == KC - 1))
                                return ins
                            P.op("tensor", lambda e, wg=wg, psg=psg, mm=mm: mm(e, wg, psg), reads=[bwg, b_h3], writes=[bpg])
                            P.op("tensor", lambda e, wu=wu, psu=psu, mm=mm: mm(e, wu, psu), reads=[bwu, b_h3], writes=[bpu])
                            st, bst = c.stage.next()
                            P.op("scalar", lambda e, st=st, psg=psg: e.activation(out=st[:], in_=psg[:], func=AF.Silu),
                                 reads=[bpg], writes=[bst])
                            P.op("vector", lambda e, st=st, psu=psu, j=j, tt=tt: e.tensor_tensor(
                                out=aT[:, j, tt * 512:(tt + 1) * 512], in0=psu[:], in1=st[:], op=ALU.mult),
                                reads=[bpu, bst], mwrites=[b_aT])
                pre, epi = make_resid(c, T, B, xin, xout, tok0)
                linear_fm(c, aT, b_aT, FC, TB, T["w_down"][l], (0, D), epi, pre=pre)
                P.barrier()


def setup_mem(c, T, B):
    nc = c.nc
    c.memn = nc.alloc_sbuf_tensor("memn", [128, KC, NMEM], BF16)
    c.b_memn = Buf("memn")
    rmsnorm_fm(c, T["memT"], B["memT"], 0, NMEM, c.pg[:, 0:16], c.memn, c.b_memn)


def phase_xattn(c, T, B, l, xin, xout):
    nc, P = c.nc, c.P
    with sbt(nc, "kTm", [128, KC, NMEM], BF16) as kTm, sbt(nc, "Vm", [128, 2, D], BF16) as Vm, \
            sbt(nc, "pT0", [128, 512], BF16) as pT0, sbt(nc, "pT1", [128, 512], BF16) as pT1, \
            sbt(nc, "pT2", [128, 512], BF16) as pT2, sbt(nc, "pT3", [128, 512], BF16) as pT3, \
            sbt(nc, "rden0", [128, 512], F32) as rd0, sbt(nc, "rden1", [128, 512], F32) as rd1:
        c.kTm, c.Vm = kTm, Vm
        c.b_kTm, c.b_Vm = Buf("kTm"), Buf("Vm")
        c.pT = Pool([(t, Buf()) for t in (pT0, pT1, pT2, pT3)])
        c.rden = Pool([(t, Buf()) for t in (rd0, rd1)])
        _phase_xattn(c, T, B, l, xin, xout)


def _phase_xattn(c, T, B, l, xin, xout):
    nc, P = c.nc, c.P
    Wkv = T["w_xkv"][l]

    def epi_k(ci, w, tt, ps, bps, tok):
        eng = c.evac_eng()
        if eng == "scalar":
            P.op("scalar", lambda e: e.copy(out=c.kTm[:w, ci // 128, :], in_=ps), reads=[bps], mwrites=[c.b_kTm])
        else:
            P.op("vector", lambda e: e.tensor_copy(out=c.kTm[:w, ci // 128, :], in_=ps), reads=[bps], mwrites=[c.b_kTm])
    linear_fm(c, c.memn, c.b_memn, KC, NMEM, Wkv, (0, D), epi_k, tw=NMEM)

    def epi_v(off, mw, tc, ps, bps):
        eng = c.evac_eng()
        if eng == "scalar":
            P.op("scalar", lambda e: e.copy(out=c.Vm[:, tc, off:off + mw], in_=ps), reads=[bps], mwrites=[c.b_Vm])
        else:
            P.op("vector", lambda e: e.tensor_copy(out=c.Vm[:, tc, off:off + mw], in_=ps), reads=[bps], mwrites=[c.b_Vm])
    linear_tm(c, c.memn, c.b_memn, KC, NMEM, Wkv, (D, D), epi_v)

    scale = 512.0 ** -0.5
    for blk in range(L // TB):
        tok0 = blk * TB
        with sbt(nc, "qTs", [128, KC, TB], BF16) as qTs:
            b_at, b_q = Buf("attnT"), Buf("qTs")
            with sbt(nc, "hq", [128, KC, TB], BF16) as hq:
                b_hq = Buf("hq")
                rmsnorm_fm(c, T[xin], B[xin], tok0, TB, c.pp[:, l, PP_NG + 16:PP_NG + 32], hq, b_hq)

                def epi_q(ci, w, tt, ps, bps, tok):
                    eng = c.evac_eng()
                    dst = qTs[:w, ci // 128, tt * 512:(tt + 1) * 512]
                    if eng == "scalar":
                        P.op("scalar", lambda e: e.mul(out=dst, in_=ps, mul=scale), reads=[bps], mwrites=[b_q])
                    else:
                        P.op("vector", lambda e: e.tensor_scalar_mul(out=dst, in0=ps, scalar1=scale), reads=[bps], mwrites=[b_q])
                linear_fm(c, hq, b_hq, KC, TB, T["w_xq"][l], (0, D), epi_q)
                P.barrier()
            _xattn_core(c, T, B, l, xin, xout, tok0, qTs, b_q)


def _xattn_core(c, T, B, l, xin, xout, tok0, qTs, b_q):
    nc, P = c.nc, c.P
    if True:
        with sbt(nc, "attnT", [128, KC, TB], BF16) as attnT:
            b_at = Buf("attnT")
            for tt in range(TB // 512):
                ts = slice(tt * 512, (tt + 1) * 512)
                for h in range(4):
                    pts = []
                    for mc in range(2):
                        ps, bps = c.psum.next()

                        def mm(e, ps=ps, mc=mc, h=h, ts=ts):
                            for dc in range(4):
                                ins = e.matmul(ps[:, :], c.kTm[:, h * 4 + dc, mc * 128:(mc + 1) * 128], qTs[:, h * 4 + dc, ts],
                                               start=(dc == 0), stop=(dc == 3))
                            return ins
                        P.op("tensor", mm, reads=[c.b_kTm, b_q], writes=[bps])
                        pt, bpt = c.pT.next()
                        P.op("scalar", lambda e, pt=pt, ps=ps: e.activation(out=pt[:], in_=ps[:], func=AF.Exp), reads=[bps], writes=[bpt])
                        pts.append((pt, bpt))
                    psd, bpd = c.psum.next()

                    def mmd(e, psd=psd, pts=pts):
                        for mc in range(2):
                            ins = e.matmul(psd[:, :], c.ones_b[:], pts[mc][0][:], start=(mc == 0), stop=(mc == 1))
                        return ins
                    P.op("tensor", mmd, reads=[pts[0][1], pts[1][1], c.b_const], writes=[bpd])
                    rd, brd = c.rden.next()
                    P.op("vector", lambda e, rd=rd, psd=psd: e.reciprocal(out=rd[:], in_=psd[:]), reads=[bpd], writes=[brd])
                    for dc in range(4):
                        pso, bpo = c.psum.next()

                        def mmo(e, pso=pso, pts=pts, h=h, dc=dc):
                            for mc in range(2):
                                ins = e.matmul(pso[:, :], c.Vm[:, mc, (h * 4 + dc) * 128:(h * 4 + dc + 1) * 128], pts[mc][0][:],
                                               start=(mc == 0), stop=(mc == 1))
                            return ins
                        P.op("tensor", mmo, reads=[c.b_Vm, pts[0][1], pts[1][1]], writes=[bpo])
                        P.op("vector", lambda e, pso=pso, rd=rd, h=h, dc=dc, ts=ts: e.tensor_tensor(
                            out=attnT[:, h * 4 + dc, ts], in0=pso[:], in1=rd[:], op=ALU.mult), reads=[bpo, brd], mwrites=[b_at])
            pre, epi = make_resid(c, T, B, xin, xout, tok0)
            linear_fm(c, attnT, b_at, KC, TB, T["w_xo"][l], (0, D), epi, pre=pre)
            P.barrier()


GN = 4224
NF = 8192


def host_consts():
    a = np.arange(GN, dtype=np.int64)
    m = (a[:, None] * a[None, :]) % NF
    ang = m.astype(np.float64) * (2.0 * np.pi / NF)
    Gc = np.cos(ang).astype(ml_dtypes.bfloat16)
    Gs = np.sin(ang).astype(ml_dtypes.bfloat16)
    ch = np.arange(128, dtype=np.int64)
    angc = ((ch[:, None] * ch[None, :]) % 128).astype(np.float64) * (2.0 * np.pi / 128)
    sc = 1.0 / math.sqrt(4096.0 * 128.0)
    csc = np.concatenate([np.cos(angc) * sc, -np.sin(angc) * sc], axis=1).astype(np.float32)
    sgn = np.tile(np.array([1.0, -1.0], np.float32), 256)[None, :].repeat(128, 0)
    return {"Gc": Gc, "Gs": Gs, "csc": csc, "sgn": np.ascontiguousarray(sgn)}


def declare_consts(nc, T, B):
    T["Gc"] = nc.dram_tensor("Gc", [GN, GN], BF16, kind="ExternalInput").ap()
    T["Gs"] = nc.dram_tensor("Gs", [GN, GN], BF16, kind="ExternalInput").ap()
    T["csc"] = nc.dram_tensor("csc", [128, 256], F32, kind="ExternalInput").ap()
    T["sgn"] = nc.dram_tensor("sgn", [128, 512], F32, kind="ExternalInput").ap()
    for n in ("Gc", "Gs", "csc", "sgn"):
        B[n] = Buf(n)


def group_norm_store(c, T, B, y4, b_y4, gcols, row0, t0, tmp, w=512):
    P = c.P
    sq, bsq = tmp["sq"].next()
    rs, brs = tmp["rs"].next()
    P.op("scalar", lambda e: e.activation(out=sq[:, :, :w], in_=y4[:, :, :w], func=AF.Square), reads=[b_y4], writes=[bsq])
    ps, bps = c.psum.next()

    def mm(e):
        for g in range(4):
            ins = e.matmul(ps[:, :w], c.ones_b[:], sq[:, g, :w], start=(g == 0), stop=(g == 3))
        return ins
    P.op("tensor", mm, reads=[bsq, c.b_const], writes=[bps])
    P.op("scalar", lambda e: e.activation(out=rs[:, :w], in_=ps[:, :w], func=AF.Sqrt, bias=c.eps_col[:, 0:1], scale=1.0 / 512),
         reads=[bps, c.b_const], writes=[brs])
    P.op("vector", lambda e: e.reciprocal(out=rs[:, :w], in_=rs[:, :w]), reads=[brs], writes=[brs])
    for g in range(4):
        st, bst = c.stage.next()
        sv = st[:].bitcast(BF16)[:, :w]
        P.op("vector", lambda e, g=g, sv=sv: e.scalar_tensor_tensor(out=sv, in0=y4[:, g, :w], scalar=gcols[:, g:g + 1], in1=rs[:, :w],
                                                                  op0=ALU.mult, op1=ALU.mult),
             reads=[b_y4, brs, c.b_const], writes=[bst])
        r = row0 + g * 128
        P.dma("sync", T["mixT"][r:r + 128, t0:t0 + w], sv, reads=[bst], mwrites=[B["mixT"]])


def gn_tmp(nc, stack, w=512):
    sq = [stack.enter_context(sbt(nc, "gsq", [128, 4, w], BF16)) for _ in range(2)]
    rs = [stack.enter_context(sbt(nc, "grs", [128, w], F32)) for _ in range(2)]
    return {"sq": Pool([(t, Buf()) for t in sq]), "rs": Pool([(t, Buf()) for t in rs])}


from contextlib import ExitStack


def phase_shortconv(c, T, B, l):
    nc, P = c.nc, c.P
    us = T["usT"]
    with ExitStack() as stack:
        tmp = gn_tmp(nc, stack)
        cts = Pool([(stack.enter_context(sbt(nc, "sc_c", [128, 514], F32)), Buf()) for _ in range(3)])
        hts = Pool([(stack.enter_context(sbt(nc, "sc_h", [128, 514], F32)), Buf()) for _ in range(3)])
        bts = Pool([(stack.enter_context(sbt(nc, "sc_b", [128, 512], F32)), Buf()) for _ in range(3)])
        y4s = Pool([(stack.enter_context(sbt(nc, "sc_y", [128, 4, 512], F32)), Buf()) for _ in range(2)])
        for tt in range(L // 512):
            t0 = tt * 512
            lo = max(t0 - 1, 0)
            hi = min(t0 + 513, L)
            a = lo - (t0 - 1)
            n = hi - lo
            y4, by4 = y4s.next()
            for cc in range(4):
                ct, bc = cts.next()
                ht, bh = hts.next()
                bt, bb = bts.next()
                wcol = c.pp[:, l, PP_SCW + cc * 3:PP_SCW + cc * 3 + 3]
                if a > 0 or n < 514:
                    P.op("vector", lambda e, ct=ct: e.memset(ct[:], 0.0), writes=[bc])
                P.dma("sync", ct[:, a:a + n], us[512 + cc * 128:512 + (cc + 1) * 128, lo:hi], reads=[B["usT"]], writes=[bc])
                P.dma("sync", ht[:, a:a + n], us[1024 + cc * 128:1024 + (cc + 1) * 128, lo:hi], reads=[B["usT"]], writes=[bh])
                P.dma("sync", bt[:], us[cc * 128:(cc + 1) * 128, t0:t0 + 512], reads=[B["usT"]], writes=[bb])
                P.op("vector", lambda e, ct=ct, ht=ht, a=a, n=n: e.tensor_tensor(out=ct[:, a:a + n], in0=ct[:, a:a + n], in1=ht[:, a:a + n], op=ALU.mult),
                     reads=[bc, bh], writes=[bc])
                yv = y4[:, cc, :]
                P.op("vector", lambda e, ct=ct, yv=yv, wcol=wcol: e.tensor_scalar_mul(out=yv, in0=ct[:, 1:513], scalar1=wcol[:, 1:2]),
                     reads=[bc, c.b_const], writes=[by4])
                P.op("vector", lambda e, ct=ct, yv=yv, wcol=wcol: e.scalar_tensor_tensor(out=yv, in0=ct[:, 0:512], scalar=wcol[:, 0:1], in1=yv,
                                                                                      op0=ALU.mult, op1=ALU.add), reads=[bc, c.b_const, by4], writes=[by4])
                P.op("vector", lambda e, ct=ct, yv=yv, wcol=wcol: e.scalar_tensor_tensor(out=yv, in0=ct[:, 2:514], scalar=wcol[:, 2:3], in1=yv,
                                                                                      op0=ALU.mult, op1=ALU.add), reads=[bc, c.b_const, by4], writes=[by4])
                P.op("vector", lambda e, yv=yv, bt=bt: e.tensor_tensor(out=yv, in0=yv, in1=bt[:], op=ALU.mult), reads=[bb, by4], writes=[by4])
            group_norm_store(c, T, B, y4, by4, c.pp[:, l, PP_GRPG + 8:PP_GRPG + 12], 1536, t0, tmp)
    P.barrier()


def phase_fnet(c, T, B, l):
    nc, P = c.nc, c.P
    Gce = T["Gc"].rearrange("(a two) b -> a two b", two=2)
    Gse = T["Gs"].rearrange("(a two) b -> a two b", two=2)
    with ExitStack() as stack:
        AB = stack.enter_context(sbt(nc, "fn_AB", [128, 4, 32, 256], BF16))
        b_AB = Buf("AB")
        csc32 = stack.enter_context(sbt(nc, "fn_csc32", [128, 256], F32))
        csc = stack.enter_context(sbt(nc, "fn_csc", [128, 256], BF16))
        sgn = stack.enter_context(sbt(nc, "fn_sgn", [128, 512], F32))
        b_k = Buf("fnconst")
        P.dma("sync", csc32[:], T["csc"], writes=[b_k])
        P.op("vector", lambda e: e.tensor_copy(out=csc[:], in_=csc32[:]), reads=[b_k], writes=[b_k])
        b_sg = Buf("sgn")
        P.dma("sync", sgn[:], T["sgn"], writes=[b_sg])
        with sbt(nc, "fn_u", [128, 4, L], BF16) as u:
            b_u = Buf("u")
            for g in range(4):
                P.dma("sync", u[:, g, :], T["ufT"][g * 128:(g + 1) * 128, :], reads=[B["ufT"]], mwrites=[b_u])
            for g in range(4):
                for pc in range(32):
                    ps, bps = c.psum.next()
                    P.op("tensor", lambda e, ps=ps, g=g, pc=pc: e.matmul(ps[:, :256], u[:, g, pc * 128:(pc + 1) * 128], csc[:],
                                                                      start=True, stop=True), reads=[b_u, b_k], writes=[bps])
                    eng = c.evac_eng()
                    if eng == "scalar":
                        P.op("scalar", lambda e, ps=ps, g=g, pc=pc: e.copy(out=AB[:, g, pc, :], in_=ps[:, :256]), reads=[bps], mwrites=[b_AB])
                    else:
                        P.op("vector", lambda e, ps=ps, g=g, pc=pc: e.tensor_copy(out=AB[:, g, pc, :], in_=ps[:, :256]), reads=[bps], mwrites=[b_AB])
            P.barrier()
        tmp = gn_tmp(nc, stack)
        FW = 256
        gts = Pool([(stack.enter_context(sbt(nc, "fn_g", [128, 16, FW], BF16)), Buf()) for _ in range(4)])
        y4s = Pool([(stack.enter_context(sbt(nc, "fn_y", [128, 4, FW], F32)), Buf()) for _ in range(2)])
        vts = Pool([(stack.enter_context(sbt(nc, "fn_v", [128, FW], F32)), Buf()) for _ in range(2)])
        for pt in range(L // FW):
            t0 = pt * FW
            gc, bgc = gts.next()
            gs, bgs = gts.next()
            P.dma("sync", gc[:], Gce[0:2048, 0, t0:t0 + FW].rearrange("(k p) b -> p k b", p=128), reads=[B["Gc"]], writes=[bgc])
            P.dma("sync", gs[:], Gse[0:2048, 0, t0:t0 + FW].rearrange("(k p) b -> p k b", p=128), reads=[B["Gs"]], writes=[bgs])
            y4, by4 = y4s.next()
            for g in range(4):
                psu, bpu = c.psum.next()
                psv, bpv = c.psum.next()

                def mm(e, ps, half, g=g, gc=gc, gs=gs):
                    for sc in range(16):
                        e.matmul(ps[:, :FW], AB[:, g, half * 16 + sc, 0:128], gc[:, sc, :], start=(sc == 0), stop=False)
                        ins = e.matmul(ps[:, :FW], AB[:, g, half * 16 + sc, 128:256], gs[:, sc, :], start=False, stop=(sc == 15))
                    return ins
                P.op("tensor", lambda e, mm=mm, psu=psu: mm(e, psu, 0), reads=[b_AB, bgc, bgs], writes=[bpu])
                P.op("tensor", lambda e, mm=mm, psv=psv: mm(e, psv, 1), reads=[b_AB, bgc, bgs], writes=[bpv])
                vt, bvt = vts.next()
                P.op("vector", lambda e, vt=vt, psv=psv: e.tensor_tensor(out=vt[:, :FW], in0=psv[:, :FW], in1=sgn[:, :FW], op=ALU.mult),
                     reads=[bpv, b_sg], writes=[bvt])
                P.op("vector", lambda e, vt=vt, psu=psu, y4=y4, g=g: e.tensor_tensor(out=y4[:, g, :FW], in0=psu[:, :FW], in1=vt[:, :FW], op=ALU.add),
                     reads=[bpu, bvt], writes=[by4])
            group_norm_store(c, T, B, y4, by4, c.pp[:, l, PP_GRPG:PP_GRPG + 4], 512, t0, tmp, w=FW)
    P.barrier()


HY_SCR = [("hcT", [1536, L], F32), ("zt0", [L, 512], BF16), ("zt1", [L, 512], BF16), ("zT1", [512, L], F32),
          ("hfs", [2, 2, L, 512], BF16), ("Hs", [2, 2, GN, 512], F32), ("Ys", [2, GN, 512], BF16)]
NFC = GN // 128


def hy_consts():
    t = np.arange(L, dtype=np.float32) / np.float32(L)
    f = np.linspace(1e-4, 15, 16, dtype=np.float32)
    ang = (np.float32(2.0 * math.pi) * t)[:, None] * f[None, :]
    feats = np.concatenate([t[:, None], np.cos(ang), -np.sin(ang)], axis=-1).astype(np.float32)
    negt = (-(t.astype(np.float64))).astype(np.float32).reshape(32, 128).T
    wf = np.zeros(GN, np.float64)
    wf[:4097] = 2.0 / NF
    wf[0] = 1.0 / NF
    wf[4096] = 1.0 / NF
    wfc = wf.astype(np.float32).reshape(NFC, 128).T
    hk = np.zeros((128, 100), np.float32)
    hk[:, 0:32] = negt
    hk[:, 32:32 + NFC] = wfc
    hk[:, 65:65 + NFC] = -wfc
    return {"featsT": np.ascontiguousarray(feats.T), "hk": hk, "ident": np.eye(128, dtype=np.float32)}


def declare_hy(nc, T, B, dbg_out=()):
    T["featsT"] = nc.dram_tensor("featsT", [33, L], F32, kind="ExternalInput").ap()
    T["hk"] = nc.dram_tensor("hk", [128, 100], F32, kind="ExternalInput").ap()
    T["ident"] = nc.dram_tensor("ident", [128, 128], F32, kind="ExternalInput").ap()
    for n, shp, dt in HY_SCR:
        T[n] = nc.dram_tensor(n, shp, dt, kind="ExternalOutput" if n in dbg_out else "Internal").ap()
    for n in ["featsT", "hk", "ident"] + [x[0] for x in HY_SCR]:
        B[n] = Buf(n)


def setup_ident(c, T, B):
    nc, P = c.nc, c.P
    c.ident = nc.alloc_sbuf_tensor("ident_b", [128, 128], BF16)
    c.hk = nc.alloc_sbuf_tensor("hk_sb", [128, 100], F32)
    c.mpi = nc.alloc_sbuf_tensor("mpi_col", [128, 1], F32)
    P.dma("gpsimd", c.ident[:], T["ident"], mwrites=[c.b_const])
    P.dma("sync", c.hk[:], T["hk"], mwrites=[c.b_const])
    P.op("vector", lambda e: e.memset(c.mpi[:], -math.pi), mwrites=[c.b_const])


def tok_major_store(c, B, src, b_src, dst, bdst, t0):
    P = c.P
    for tl in range(4):
        ps, bps = c.psum.next()
        pb = ps[:].bitcast(BF16)

        def tr(e, pb=pb, tl=tl):
            for cc in range(4):
                ins = e.transpose(out=pb[:, cc * 128:(cc + 1) * 128], in_=src[:, cc, tl * 128:(tl + 1) * 128], identity=c.ident[:])
            return ins
        P.op("tensor", tr, reads=[b_src, c.b_const], writes=[bps])
        st, bst = c.stage.next()
        sv = st[:].bitcast(BF16)[:, :512]
        copy_op(c, c.evac_eng(), sv, pb[:, :512], [bps], [bst])
        r = t0 + tl * 128
        P.dma("sync", dst[r:r + 128, :], sv, reads=[bst], mwrites=[bdst])


def hy_filter_mlp(c, T, B, l):
    nc, P = c.nc, c.P
    with ExitStack() as st_:
        E = st_.enter_context
        fe = E(sbt(nc, "hy_fe", [33, L], F32))
        h1 = E(sbt(nc, "hy_h1", [64, L], F32))
        h2 = E(sbt(nc, "hy_h2", [64, L], F32))
        w1 = E(sbt(nc, "hy_w1", [33, 64], F32))
        w2 = E(sbt(nc, "hy_w2", [64, 64], F32))
        w3 = E(sbt(nc, "hy_w3", [64, 2048], F32))
        fb = E(sbt(nc, "hy_fb", [64, 2], F32))
        dec = E(sbt(nc, "hy_dec", [128, 2048], F32))
        arg = [(E(sbt(nc, "hy_arg", [64, 512], F32)), Buf()) for _ in range(6)]
        argp = Pool(arg)
        wins = Pool([(E(sbt(nc, "hy_win", [128, 2048], F32)), Buf()) for _ in range(2)])
        hws = Pool([(E(sbt(nc, "hy_hw", [128, 2048], F32)), Buf()) for _ in range(2)])
        outs = Pool([(E(sbt(nc, "hy_o", [128, 2, 2, 512], BF16)), Buf()) for _ in range(2)])
        bk = Buf("hyk")
        P.dma("sync", fe[:], T["featsT"], mwrites=[bk])
        P.dma("sync", w1[:], T["hw1"][l], mwrites=[bk])
        P.dma("sync", w2[:], T["hw2"][l], mwrites=[bk])
        P.dma("sync", w3[:], T["hw3"][l], mwrites=[bk])
        P.dma("sync", dec[:], T["hdec"][l].partition_broadcast(128), mwrites=[bk])
        bk2 = Buf("hyk2")
        P.op("scalar", lambda e: e.activation(out=dec[:], in_=dec[:], func=AF.Abs), reads=[bk], writes=[bk2])
        ppl = c.pp[:, l, :]
        P.op("vector", lambda e: e.tensor_tensor(out=fb[:, 0:1], in0=ppl[:64, PP_HB1:PP_HB1 + 1], in1=ppl[:64, PP_HFREQ:PP_HFREQ + 1], op=ALU.mult),
             reads=[c.b_const], writes=[bk2])
        P.op("vector", lambda e: e.tensor_tensor(out=fb[:, 1:2], in0=ppl[:64, PP_HB2:PP_HB2 + 1], in1=ppl[:64, PP_HFREQ:PP_HFREQ + 1], op=ALU.mult),
             reads=[c.b_const, bk2], writes=[bk2])
        b_h1, b_h2 = Buf("h1"), Buf("h2")

        def sin_layer(wt, kin, src, b_srcs, dst, b_dst, j):
            for tt in range(L // 512):
                ts = slice(tt * 512, (tt + 1) * 512)
                ps, bps = c.psum.next()
                P.op("tensor", lambda e, ps=ps, ts=ts: e.matmul(ps[:64, :], wt[:kin, :], src[:kin, ts], start=True, stop=True),
                     reads=b_srcs, writes=[bps])
                a, ba = argp.next()
                P.op("vector", lambda e, ps=ps, a=a: e.tensor_scalar(out=a[:], in0=ps[:64, :], scalar1=ppl[:64, PP_HFREQ:PP_HFREQ + 1],
                                                                  scalar2=fb[:, j:j + 1], op0=ALU.mult, op1=ALU.add),
                     reads=[bps, bk2, c.b_const], writes=[ba])
                m1, bm1 = argp.next()
                m2, bm2 = argp.next()
                P.op("vector", lambda e, a=a, m1=m1: e.tensor_scalar(out=m1[:], in0=a[:], scalar1=math.pi, scalar2=-2.0 * math.pi,
                                                                  op0=ALU.is_gt, op1=ALU.mult), reads=[ba], writes=[bm1])
                P.op("vector", lambda e, a=a, m2=m2: e.tensor_scalar(out=m2[:], in0=a[:], scalar1=-math.pi, scalar2=2.0 * math.pi,
                                                                  op0=ALU.is_lt, op1=ALU.mult), reads=[ba], writes=[bm2])
                P.op("vector", lambda e, m1=m1, m2=m2: e.tensor_tensor(out=m1[:], in0=m1[:], in1=m2[:], op=ALU.add), reads=[bm1, bm2], writes=[bm1])
                P.op("vector", lambda e, a=a, m1=m1: e.tensor_tensor(out=a[:], in0=a[:], in1=m1[:], op=ALU.add), reads=[ba, bm1], writes=[ba])
                P.op("scalar", lambda e, a=a, ts=ts: e.activation(out=dst[:, ts], in_=a[:], func=AF.Sin),
                     reads=[ba, c.b_const], mwrites=[b_dst])
        sin_layer(w1, 33, fe, [bk], h1, b_h1, 0)
        sin_layer(w2, 64, h1, [bk, b_h1], h2, b_h2, 1)
        for tc in range(32):
            win, bw = wins.next()
            hw, bhw = hws.next()
            P.op("scalar", lambda e, win=win, tc=tc: e.activation(out=win[:], in_=dec[:], func=AF.Exp, scale=c.hk[:, tc:tc + 1]),
                 reads=[bk2, c.b_const], writes=[bw])
            for n4 in range(4):
                ps, bps = c.psum.next()
                P.op("tensor", lambda e, ps=ps, tc=tc, n4=n4: e.matmul(ps[:, :], h2[:, tc * 128:(tc + 1) * 128], w3[:, n4 * 512:(n4 + 1) * 512],
                                                                    start=True, stop=True), reads=[b_h2, bk], writes=[bps])
                P.op("vector", lambda e, ps=ps, hw=hw, win=win, n4=n4: e.tensor_tensor(
                    out=hw[:, n4 * 512:(n4 + 1) * 512], in0=ps[:, :], in1=win[:, n4 * 512:(n4 + 1) * 512], op=ALU.mult),
                    reads=[bps, bw], writes=[bhw])
            if tc == 0:
                for o in range(2):
                    P.op("vector", lambda e, hw=hw, o=o: e.memset(hw[0:1, o * 1024 + 512:o * 1024 + 1024], 0.0), reads=[bhw], writes=[bhw])
            ot, bo = outs.next()
            hv = hw[:].rearrange("p (o d c) -> p o d c", o=2, d=2)
            P.op("vector", lambda e, ot=ot, hv=hv: e.tensor_tensor(out=ot[:, :, 0, :], in0=hv[:, :, 0, :], in1=hv[:, :, 1, :], op=ALU.add),
                 reads=[bhw], writes=[bo])
            P.op("gpsimd", lambda e, ot=ot, hv=hv: e.tensor_tensor(out=ot[:, :, 1, :], in0=hv[:, :, 0, :], in1=hv[:, :, 1, :], op=ALU.subtract),
                 reads=[bhw, bo], writes=[bo])
            for o in range(2):
                for sd in range(2):
                    P.dma("sync", T["hfs"][o, sd, tc * 128:(tc + 1) * 128, :], ot[:, o, sd, :], reads=[bo], mwrites=[B["hfs"]])
    P.barrier()


def hy_fwd_dft(c, T, B, z_src, b_z, parts, on_chunk):
    nc, P = c.nc, c.P
    with ExitStack() as st_:
        E = st_.enter_context
        z = E(sbt(nc, "hy_z", [128, 32, 512], BF16))
        gt = Pool([(E(sbt(nc, "hy_gt", [128, 32, 256], BF16)), Buf()) for _ in range(4)])
        bz = Buf("z")
        for q in range(4):
            P.dma("sync", z[:, q * 8:(q + 1) * 8, :], z_src[q * 1024:(q + 1) * 1024, :].rearrange("(k p) c -> p k c", p=128),
                  reads=[b_z], mwrites=[bz])
        Gv = {"c": T["Gc"][0:L, :].rearrange("(k p) f -> p k f", p=128), "s": T["Gs"][0:L, :].rearrange("(k p) f -> p k f", p=128)}
        Gb = {"c": B["Gc"], "s": B["Gs"]}
        for f0 in range(0, NFC, 2):
            nf = min(2, NFC - f0)
            g = {}
            for p_ in parts:
                gtile, bg = gt.next()
                P.dma("sync", gtile[:, :, :nf * 128], Gv[p_][:, :, f0 * 128:(f0 + nf) * 128], reads=[Gb[p_]], writes=[bg])
                g[p_] = (gtile, bg)
            for j in range(nf):
                res = {}
                for p_ in parts:
                    ps, bps = c.psum.next()
                    gtile, bg = g[p_]

                    def mm(e, ps=ps, gtile=gtile, j=j):
                        for tc in range(32):
                            ins = e.matmul(ps[:, :], gtile[:, tc, j * 128:(j + 1) * 128], z[:, tc, :], start=(tc == 0), stop=(tc == 31))
                        return ins
                    P.op("tensor", mm, reads=[bg, bz], writes=[bps])
                    res[p_] = (ps, bps)
                on_chunk(f0 + j, res)
    P.barrier()


def hy_spectrum(c, T, B, o):
    P = c.P
    for which, part, cb in ((0, "c", 32), (1, "s", 65)):
        def on_chunk(fc, res, which=which, part=part, cb=cb):
            ps, bps = res[part]
            col = cb + fc
            st, bst = c.stage.next()
            if fc % 2 == 0:
                P.op("vector", lambda e: e.tensor_scalar_mul(out=st[:], in0=ps[:, :], scalar1=c.hk[:, col:col + 1]),
                     reads=[bps, c.b_const], writes=[bst])
            else:
                P.op("scalar", lambda e: e.mul(out=st[:], in_=ps[:, :], mul=c.hk[:, col:col + 1]),
                     reads=[bps, c.b_const], writes=[bst])
            P.dma("sync", T["Hs"][o, which, fc * 128:(fc + 1) * 128, :], st[:], reads=[bst], mwrites=[B["Hs"]])
        hy_fwd_dft(c, T, B, T["hfs"][o, which], B["hfs"], (part,), on_chunk)


def hy_conv(c, T, B, l, o, zsrc, make_epilogue):
    nc, P = c.nc, c.P
    with ExitStack() as st_:
        E = st_.enter_context
        hts = Pool([(E(sbt(nc, "hy_ht", [128, 2, 512], F32)), Buf()) for _ in range(2)])
        tms = Pool([(E(sbt(nc, "hy_tm", [128, 512], F32)), Buf()) for _ in range(4)])
        ybs = Pool([(E(sbt(nc, "hy_yb", [128, 2, 512], BF16)), Buf()) for _ in range(2)])

        def on_chunk(fc, res):
            psc, bpc = res["c"]
            pss, bpss = res["s"]
            ht, bh = hts.next()
            P.dma("sync", ht[:], T["Hs"][o, :, fc * 128:(fc + 1) * 128, :].rearrange("w p c -> p w c"), reads=[B["Hs"]], writes=[bh])
            t1, b1 = tms.next()
            t2, b2 = tms.next()
            t3, b3 = tms.next()
            t4, b4 = tms.next()
            yb, byb = ybs.next()
            P.op("vector", lambda e: e.tensor_tensor(out=t1[:], in0=psc[:, :], in1=ht[:, 0, :], op=ALU.mult), reads=[bpc, bh], writes=[b1])
            P.op("vector", lambda e: e.tensor_tensor(out=t2[:], in0=pss[:, :], in1=ht[:, 1, :], op=ALU.mult), reads=[bpss, bh], writes=[b2])
            P.op("vector", lambda e: e.tensor_tensor(out=t3[:], in0=pss[:, :], in1=ht[:, 0, :], op=ALU.mult), reads=[bpss, bh], writes=[b3])
            P.op("vector", lambda e: e.tensor_tensor(out=t4[:], in0=psc[:, :], in1=ht[:, 1, :], op=ALU.mult), reads=[bpc, bh], writes=[b4])
            P.op("gpsimd", lambda e: e.tensor_tensor(out=yb[:, 0, :], in0=t1[:], in1=t2[:], op=ALU.add), reads=[b1, b2], writes=[byb])
            P.op("gpsimd", lambda e: e.tensor_tensor(out=yb[:, 1, :], in0=t3[:], in1=t4[:], op=ALU.subtract), reads=[b3, b4, byb], writes=[byb])
            P.dma("sync", T["Ys"][:, fc * 128:(fc + 1) * 128, :].rearrange("w p c -> p w c"), yb[:], reads=[byb], mwrites=[B["Ys"]])
        hy_fwd_dft(c, T, B, zsrc[0], zsrc[1], ("c", "s"), on_chunk)
    FG = 8
    with ExitStack() as st_:
        E = st_.enter_context
        epilogue = make_epilogue(st_)
        yts = Pool([(E(sbt(nc, "hy_yt", [128, 2, FG, 512], BF16)), Buf()) for _ in range(2)])
        gts = Pool([(E(sbt(nc, "hy_gi", [128, 2, FG, 512], BF16)), Buf()) for _ in range(2)])
        Ysv = T["Ys"].rearrange("w (k p) c -> p w k c", p=128)
        Gcv = T["Gc"].rearrange("(k p) t -> p k t", p=128)
        Gsv = T["Gs"].rearrange("(k p) t -> p k t", p=128)
        for tt in range(L // 512):
            pss = [c.psum.next() for _ in range(4)]
            for g0 in range(0, NFC, FG):
                ng = min(FG, NFC - g0)
                yt, byt = yts.next()
                gt, bgt = gts.next()
                for w_ in range(2):
                    P.dma("sync", yt[:, w_, :ng, :], Ysv[:, w_, g0:g0 + ng, :], reads=[B["Ys"]], mwrites=[byt])
                P.dma("sync", gt[:, 0, :ng, :], Gcv[:, g0:g0 + ng, tt * 512:(tt + 1) * 512], reads=[B["Gc"]], mwrites=[bgt])
                P.dma("sync", gt[:, 1, :ng, :], Gsv[:, g0:g0 + ng, tt * 512:(tt + 1) * 512], reads=[B["Gs"]], mwrites=[bgt])

                def mm(e, yt=yt, gt=gt, g0=g0, ng=ng, pss=pss):
                    for cc in range(4):
                        for k in range(ng):
                            for w_ in range(2):
                                ins = e.matmul(pss[cc][0][:, :], yt[:, w_, k, cc * 128:(cc + 1) * 128], gt[:, w_, k, :],
                                               start=(g0 == 0 and k == 0 and w_ == 0), stop=(g0 + k == NFC - 1 and w_ == 1))
                    return ins
                P.op("tensor", mm, reads=[byt, bgt], writes=[p[1] for p in pss])
            for cc in range(4):
                epilogue(tt, cc, pss[cc][0], pss[cc][1])
    P.barrier()


def phase_hyena(c, T, B, l):
    nc, P = c.nc, c.P
    hy_filter_mlp(c, T, B, l)
    for o in range(2):
        hy_spectrum(c, T, B, o)
    uh = T["uhT"]
    with ExitStack() as st_:
        E = st_.enter_context
        uts = Pool([(E(sbt(nc, "hy_ut", [128, 514], F32)), Buf()) for _ in range(3)])
        ots = Pool([(E(sbt(nc, "hy_ot", [128, 512], F32)), Buf()) for _ in range(3)])
        vbs = Pool([(E(sbt(nc, "hy_vb", [128, 4, 512], BF16)), Buf()) for _ in range(2)])
        for tt in range(L // 512):
            t0 = tt * 512
            lo, hi = max(t0 - 1, 0), min(t0 + 513, L)
            a, n = lo - (t0 - 1), hi - lo
            vb, bvb = vbs.next()
            for ch in range(12):
                ut, bu = uts.next()
                ot, bo = ots.next()
                wcol = c.pp[:, l, PP_HCW + ch * 3:PP_HCW + ch * 3 + 3]
                if n < 514:
                    P.op("vector", lambda e, ut=ut: e.memset(ut[:], 0.0), writes=[bu])
                P.dma("sync", ut[:, a:a + n], uh[ch * 128:(ch + 1) * 128, lo:hi], reads=[B["uhT"]], writes=[bu])
                P.op("vector", lambda e, ut=ut, ot=ot, wcol=wcol: e.tensor_scalar_mul(out=ot[:], in0=ut[:, 1:513], scalar1=wcol[:, 1:2]),
                     reads=[bu, c.b_const], writes=[bo])
                P.op("vector", lambda e, ut=ut, ot=ot, wcol=wcol: e.scalar_tensor_tensor(out=ot[:], in0=ut[:, 0:512], scalar=wcol[:, 0:1], in1=ot[:],
                                                                                      op0=ALU.mult, op1=ALU.add), reads=[bu, bo, c.b_const], writes=[bo])
                P.op("vector", lambda e, ut=ut, ot=ot, wcol=wcol: e.scalar_tensor_tensor(out=ot[:], in0=ut[:, 2:514], scalar=wcol[:, 2:3], in1=ot[:],
                                                                                      op0=ALU.mult, op1=ALU.add), reads=[bu, bo, c.b_const], writes=[bo])
                P.dma("sync", T["hcT"][ch * 128:(ch + 1) * 128, t0:t0 + 512], ot[:], reads=[bo], mwrites=[B["hcT"]])
                if ch < 4:
                    P.op("scalar", lambda e, vb=vb, ot=ot, ch=ch: e.copy(out=vb[:, ch, :], in_=ot[:]), reads=[bo], mwrites=[bvb])
            tok_major_store(c, B, vb, bvb, T["zt0"], B["zt0"], t0)
    P.barrier()
    def mk0(st_):
        E = st_.enter_context
        xts = Pool([(E(sbt(nc, "hy_x", [128, 2, 512], F32)), Buf()) for _ in range(3)])
        zbs = Pool([(E(sbt(nc, "hy_zb", [128, 4, 512], BF16)), Buf()) for _ in range(2)])
        cur = {}

        def epi0(tt, cc, ps, bps):
            t0 = tt * 512
            if cc == 0:
                cur["zb"] = zbs.next()
            zb, bzb = cur["zb"]
            xt, bx = xts.next()
            P.dma("sync", xt[:, 0, :], T["hcT"][cc * 128:(cc + 1) * 128, t0:t0 + 512], reads=[B["hcT"]], mwrites=[bx])
            P.dma("sync", xt[:, 1, :], T["hcT"][512 + cc * 128:512 + (cc + 1) * 128, t0:t0 + 512], reads=[B["hcT"]], mwrites=[bx])
            sk = c.pp[:, l, PP_HSKIP + cc:PP_HSKIP + cc + 1]
            P.op("vector", lambda e: e.scalar_tensor_tensor(out=xt[:, 0, :], in0=xt[:, 0, :], scalar=sk, in1=ps[:, :], op0=ALU.mult, op1=ALU.add),
                 reads=[bps, bx, c.b_const], writes=[bx])
            P.op("vector", lambda e: e.tensor_tensor(out=xt[:, 0, :], in0=xt[:, 0, :], in1=xt[:, 1, :], op=ALU.mult), reads=[bx], writes=[bx])
            P.dma("sync", T["zT1"][cc * 128:(cc + 1) * 128, t0:t0 + 512], xt[:, 0, :], reads=[bx], mwrites=[B["zT1"]])
            P.op("scalar", lambda e: e.copy(out=zb[:, cc, :], in_=xt[:, 0, :]), reads=[bx], mwrites=[bzb])
            if cc == 3:
                tok_major_store(c, B, zb, bzb, T["zt1"], B["zt1"], t0)
        return epi0
    hy_conv(c, T, B, l, 0, (T["zt0"], B["zt0"]), mk0)

    def mk1(st_):
        E = st_.enter_context
        tmp = gn_tmp(nc, st_)
        xts = Pool([(E(sbt(nc, "hy_x", [128, 2, 512], F32)), Buf()) for _ in range(3)])
        y4s = Pool([(E(sbt(nc, "hy_y4", [128, 4, 512], F32)), Buf()) for _ in range(2)])
        cur = {}

        def epi1(tt, cc, ps, bps):
            t0 = tt * 512
            if cc == 0:
                cur["y4"] = y4s.next()
            y4, by4 = cur["y4"]
            xt, bx = xts.next()
            P.dma("sync", xt[:, 0, :], T["zT1"][cc * 128:(cc + 1) * 128, t0:t0 + 512], reads=[B["zT1"]], mwrites=[bx])
            P.dma("sync", xt[:, 1, :], T["hcT"][1024 + cc * 128:1024 + (cc + 1) * 128, t0:t0 + 512], reads=[B["hcT"]], mwrites=[bx])
            sk = c.pp[:, l, PP_HSKIP + 4 + cc:PP_HSKIP + 4 + cc + 1]
            P.op("vector", lambda e: e.scalar_tensor_tensor(out=xt[:, 0, :], in0=xt[:, 0, :], scalar=sk, in1=ps[:, :], op0=ALU.mult, op1=ALU.add),
                 reads=[bps, bx, c.b_const], writes=[bx])
            P.op("vector", lambda e: e.tensor_tensor(out=y4[:, cc, :], in0=xt[:, 0, :], in1=xt[:, 1, :], op=ALU.mult), reads=[bx, by4], writes=[by4])
            if cc == 3:
                group_norm_store(c, T, B, y4, by4, c.pp[:, l, PP_GRPG + 4:PP_GRPG + 8], 1024, t0, tmp)
        return epi1
    hy_conv(c, T, B, l, 1, (T["zt1"], B["zt1"]), mk1)


def gla_consts():
    s = np.arange(128)
    Uf = np.where(s[:, None] <= s[None, :], -1.0 / 16.0, 0.0)
    Ub = np.where(s[:, None] >= s[None, :], -1.0 / 16.0, 0.0)
    Mf = np.where(s[:, None] <= s[None, :], 1.0, 0.0)
    Mb = np.where(s[:, None] > s[None, :], 1.0, 0.0)
    g = np.zeros((128, 2, 128 + 512), np.float32)
    g[:, 0, :128] = Uf
    g[:, 1, :128] = Ub
    g[:, 0, 128:] = np.tile(Mf, (1, 4))
    g[:, 1, 128:] = np.tile(Mb, (1, 4))
    return {"glac": g}


def declare_gla(nc, T, B, dbg_out=()):
    T["glac"] = nc.dram_tensor("glac", [128, 2, 640], F32, kind="ExternalInput").ap()
    T["ofT"] = nc.dram_tensor("ofT", [512, L], F32, kind="ExternalOutput" if "ofT" in dbg_out else "Internal").ap()
    B["glac"] = Buf()
    B["ofT"] = Buf()


import os
GLA_LEVEL = int(os.environ.get("GLA_LEVEL", "99"))


def dbg_dump(c, name, ap, buf, shape, dt):
    if not getattr(c, "dbg", False):
        return
    t = c.nc.dram_tensor("dbg_" + name, shape, dt, kind="ExternalOutput").ap()
    c.P.dma("sync", t, ap, reads=[buf], store=True)


def phase_gla(c, T, B, l):
    nc, P = c.nc, c.P
    with ExitStack() as st_:
        E = st_.enter_context
        gc = E(sbt(nc, "gl_c", [128, 2, 640], F32))
        gw = E(sbt(nc, "gl_w", [32, 2, 256], F32))
        gb = E(sbt(nc, "gl_b", [1, 2, 256], F32))
        onesf = E(sbt(nc, "gl_1", [128, 128], F32))
        bk = Buf("glk")
        P.dma("sync", gc[:], T["glac"], mwrites=[bk])
        for d in range(2):
            P.dma("sync", gw[:, d, :], T["gkw"][l, d], mwrites=[bk])
            P.dma("sync", gb[:, d, :], T["gkb"][l, d:d + 1, :], mwrites=[bk])
        P.op("vector", lambda e: e.memset(onesf[:], 1.0), mwrites=[bk])

        def mk(name, shape, dt, n=2):
            return Pool([(E(sbt(nc, name, shape, dt)), Buf()) for _ in range(n)])
        qts, kts = mk("gl_q", [128, 2, 512], F32), mk("gl_k", [128, 2, 512], F32)
        lrs, vts = mk("gl_lr", [32, 512], F32), mk("gl_v", [128, 4, 512], BF16)
        gts = mk("gl_g", [128, 4, 512], F32)
        e1s, ls = mk("gl_e1", [128, 256], F32), mk("gl_l", [128, 256], F32)
        bsbs = mk("gl_bsb", [128, 2, 128], F32)
        cols = mk("gl_col", [128, 8], F32)
        Es = mk("gl_E", [128, 4, 2, 128], F32)
        qk16 = mk("gl_qk", [128, 4, 2, 128], BF16)
        qzs = mk("gl_qz", [128, 2, 2, 2, 128], BF16)
        for _ in range(2):
            qz_, bqz_ = qzs.next()
            P.op("vector", lambda e, qz_=qz_: e.memset(qz_[:], 0.0), writes=[bqz_])
        klTs = mk("gl_klT", [128, 256], BF16)
        sTms = mk("gl_sTm", [128, 512], BF16)
        S32 = E(sbt(nc, "gl_S", [128, 2, 128], F32))
        Sbf = mk("gl_Sbf", [128, 2, 128], BF16)
        ocs, sqs = mk("gl_oc", [128, 512], F32), mk("gl_sq", [128, 512], BF16)
        rss, sgs = mk("gl_rs", [128, 512], F32), mk("gl_sg", [128, 512], F32)
        ofs = mk("gl_of", [128, 512], F32)
        b_S = Buf("S")
        qv = T["qT"].rearrange("(c p) t -> p c t", p=128)
        kv = T["kT"].rearrange("(c p) t -> p c t", p=128)
        gv = T["gT"].rearrange("(h p) t -> p h t", p=128)
        ofv = T["ofT"].rearrange("(h p) t -> p h t", p=128)
        mxv = T["mixT"][0:512, :].rearrange("(h p) t -> p h t", p=128)
        for d in range(2):
            REF, LAST = (64, 127) if d == 0 else (63, 0)
            P.op("vector", lambda e: e.memset(S32[:], 0.0), writes=[b_S])
            sb0, bsb0 = Sbf.next()
            P.op("vector", lambda e, sb0=sb0: e.memset(sb0[:], 0.0), writes=[bsb0])
            cur_sbf = (sb0, bsb0)
            groups = range(8) if d == 0 else range(7, -1, -1)
            for grp in groups:
                t0 = grp * 512
                qt, bq = qts.next()
                kt, bkt = kts.next()
                lr, blr = lrs.next()
                vt, bv = vts.next()
                P.dma("sync", qt[:], qv[:, :, t0:t0 + 512], reads=[B["qT"]], writes=[bq])
                P.dma("sync", kt[:], kv[:, :, t0:t0 + 512], reads=[B["kT"]], writes=[bkt])
                P.dma("sync", lr[:], T["lrT"][:, t0:t0 + 512], reads=[B["lrT"]], writes=[blr])
                P.dma("sync", vt[:], T["vtok"][t0:t0 + 512, :].rearrange("(b p) c -> p b c", p=128), reads=[B["vtok"]], writes=[bv])
                if d == 1:
                    gt, bg = gts.next()
                    P.dma("sync", gt[:], gv[:, :, t0:t0 + 512], reads=[B["gT"]], writes=[bg])
                blocks = range(4) if d == 0 else range(3, -1, -1)
                for blk in blocks:
                    bs = slice(blk * 128, (blk + 1) * 128)
                    tb = t0 + blk * 128
                    ps, bps = c.psum.next()

                    def mm1(e, ps=ps, lr=lr, bs=bs, d=d):
                        e.matmul(ps[:, :256], lr[:, bs], gw[:, d, :], start=True, stop=False)
                        return e.matmul(ps[:, :256], onesf[0:1, :], gb[0:1, d, :], start=False, stop=True)
                    P.op("tensor", mm1, reads=[blr, bk], writes=[bps])
                    e1, be1 = e1s.next()
                    lt, bl = ls.next()
                    P.op("scalar", lambda e, e1=e1, ps=ps: e.activation(out=e1[:], in_=ps[:, :256], func=AF.Exp, scale=-1.0), reads=[bps], writes=[be1])
                    P.op("scalar", lambda e, e1=e1, lt=lt: e.activation(out=lt[:], in_=e1[:], func=AF.Ln, bias=onesf[:, 0:1]), reads=[be1, bk], writes=[bl])
                    ps2, bps2 = c.psum.next()

                    def mm2(e, ps2=ps2, lt=lt, d=d):
                        for cc in range(2):
                            ins = e.matmul(ps2[:, cc * 128:(cc + 1) * 128], lt[:, cc * 128:(cc + 1) * 128], gc[:, d, 0:128], start=True, stop=True)
                        return ins
                    P.op("tensor", mm2, reads=[bl, bk], writes=[bps2])
                    if GLA_LEVEL < 2:
                        continue
                    bsb, bbsb = bsbs.next()
                    P.op("vector", lambda e, bsb=bsb, ps2=ps2: e.tensor_copy(out=bsb[:].rearrange("p c t -> p (c t)"), in_=ps2[:, :256]), reads=[bps2], writes=[bbsb])
                    col, bcol = cols.next()
                    P.op("vector", lambda e, col=col, bsb=bsb, REF=REF: e.tensor_scalar_mul(out=col[:, 0:2], in0=bsb[:, :, REF], scalar1=-1.0), reads=[bbsb], writes=[bcol])
                    P.op("vector", lambda e, col=col, bsb=bsb, REF=REF: e.tensor_copy(out=col[:, 2:4], in_=bsb[:, :, REF]), reads=[bbsb, bcol], writes=[bcol])
                    P.op("vector", lambda e, col=col, bsb=bsb, LAST=LAST: e.tensor_copy(out=col[:, 4:6], in_=bsb[:, :, LAST]), reads=[bbsb, bcol], writes=[bcol])
                    P.op("scalar", lambda e, col=col: e.activation(out=col[:, 6:8], in_=col[:, 4:6], func=AF.Exp), reads=[bcol], writes=[bcol])
                    Et, bE = Es.next()
                    for cc in range(2):
                        P.op("scalar", lambda e, Et=Et, bsb=bsb, col=col, cc=cc: e.activation(out=Et[:, 0, cc, :], in_=bsb[:, cc, :], func=AF.Exp, bias=col[:, cc:cc + 1]),
                             reads=[bbsb, bcol, bE], writes=[bE])
                        P.op("scalar", lambda e, Et=Et, bsb=bsb, col=col, cc=cc: e.activation(out=Et[:, 1, cc, :], in_=bsb[:, cc, :], func=AF.Exp, bias=col[:, 2 + cc:3 + cc], scale=-1.0),
                             reads=[bbsb, bcol, bE], writes=[bE])
                        P.op("scalar", lambda e, Et=Et, bsb=bsb, col=col, cc=cc: e.activation(out=Et[:, 3, cc, :], in_=bsb[:, cc, :], func=AF.Exp, bias=col[:, 4 + cc:5 + cc], scale=-1.0),
                             reads=[bbsb, bcol, bE], writes=[bE])
                    P.op("scalar", lambda e, Et=Et, bsb=bsb: e.activation(out=Et[:, 2, :, :], in_=bsb[:], func=AF.Exp), reads=[bbsb, bE], writes=[bE])
                    if GLA_LEVEL < 3:
                        continue
                    qk, bqk = qk16.next()
                    qz, bqz = qzs.next()
                    for hh in range(2):
                        pr = slice(hh * 64, (hh + 1) * 64)
                        P.op("vector", lambda e, qz=qz, qt=qt, Et=Et, bs=bs, pr=pr, hh=hh: e.scalar_tensor_tensor(
                            out=qz[pr, hh, 0, :, :], in0=qt[pr, :, bs], scalar=0.125, in1=Et[pr, 0, :, :], op0=ALU.mult, op1=ALU.mult),
                            reads=[bq, bE], mwrites=[bqz])
                        P.op("vector", lambda e, qz=qz, qt=qt, Et=Et, bs=bs, pr=pr, hh=hh: e.scalar_tensor_tensor(
                            out=qz[pr, hh, 1, :, :], in0=qt[pr, :, bs], scalar=0.125, in1=Et[pr, 2, :, :], op0=ALU.mult, op1=ALU.mult),
                            reads=[bq, bE], mwrites=[bqz])
                    P.op("vector", lambda e, qk=qk, kt=kt, Et=Et, bs=bs: e.tensor_tensor(out=qk[:, 1, :, :], in0=kt[:, :, bs], in1=Et[:, 1, :, :], op=ALU.mult),
                         reads=[bkt, bE], writes=[bqk])
                    P.op("vector", lambda e, qk=qk, kt=kt, Et=Et, bs=bs: e.tensor_tensor(out=qk[:, 3, :, :], in0=kt[:, :, bs], in1=Et[:, 3, :, :], op=ALU.mult),
                         reads=[bkt, bE, bqk], writes=[bqk])
                    if GLA_LEVEL < 4:
                        continue
                    ps3, bps3 = c.psum.next()
                    pb3 = ps3[:].bitcast(BF16)

                    def tr(e, pb3=pb3, qk=qk):
                        for cc in range(2):
                            ins = e.transpose(out=pb3[:, cc * 128:(cc + 1) * 128], in_=qk[:, 3, cc, :], identity=c.ident[:])
                        return ins
                    P.op("tensor", tr, reads=[bqk, c.b_const], writes=[bps3])
                    klT, bklT = klTs.next()
                    P.op("vector", lambda e, klT=klT, pb3=pb3: e.tensor_copy(out=klT[:], in_=pb3[:, :256]), reads=[bps3], writes=[bklT])
                    ps4, bps4 = c.psum.next()

                    def mm4(e, ps4=ps4, qk=qk, qz=qz):
                        for h in range(4):
                            cc, hh = h // 2, h % 2
                            ins = e.matmul(ps4[:, h * 128:(h + 1) * 128], qk[:, 1, cc, :], qz[:, hh, 0, cc, :], start=True, stop=True)
                        return ins
                    P.op("tensor", mm4, reads=[bqk, bqz], writes=[bps4])
                    sTm, bsT = sTms.next()
                    P.op("vector", lambda e, sTm=sTm, ps4=ps4, d=d: e.tensor_tensor(out=sTm[:], in0=ps4[:, :], in1=gc[:, d, 128:640], op=ALU.mult), reads=[bps4, bk], writes=[bsT])
                    if GLA_LEVEL < 6:
                        continue
                    sbf, bsbf = cur_sbf
                    ps5, bps5 = c.psum.next()

                    def mm5(e, ps5=ps5, vt=vt, sTm=sTm, qz=qz, sbf=sbf, blk=blk):
                        for h in range(4):
                            cc, hh = h // 2, h % 2
                            e.matmul(ps5[:, h * 128:(h + 1) * 128], vt[:, blk, h * 128:(h + 1) * 128], sTm[:, h * 128:(h + 1) * 128], start=True, stop=False)
                            ins = e.matmul(ps5[:, h * 128:(h + 1) * 128], sbf[:, cc, :], qz[:, hh, 1, cc, :], start=False, stop=True)
                        return ins
                    P.op("tensor", mm5, reads=[bv, bsT, bqz, bsbf], writes=[bps5])
                    if GLA_LEVEL < 7:
                        continue
                    ps6, bps6 = c.psum.next()

                    def mm6(e, ps6=ps6, klT=klT, vt=vt, blk=blk):
                        for cc in range(2):
                            ins = e.matmul(ps6[:, cc * 256:(cc + 1) * 256], klT[:, cc * 128:(cc + 1) * 128], vt[:, blk, cc * 256:(cc + 1) * 256], start=True, stop=True)
                        return ins
                    P.op("tensor", mm6, reads=[bklT, bv], writes=[bps6])
                    for h in range(4):
                        cc, hh = h // 2, h % 2
                        pr = slice(hh * 64, (hh + 1) * 64)
                        P.op("vector", lambda e, cc=cc, hh=hh, pr=pr, col=col, ps6=ps6: e.scalar_tensor_tensor(
                            out=S32[pr, cc, :], in0=S32[pr, cc, :], scalar=col[pr, 6 + cc:7 + cc], in1=ps6[pr, cc * 256 + hh * 128:cc * 256 + (hh + 1) * 128],
                            op0=ALU.mult, op1=ALU.add), reads=[b_S, bcol, bps6], writes=[b_S])
                    nsb, bnsb = Sbf.next()
                    P.op("vector", lambda e, nsb=nsb: e.tensor_copy(out=nsb[:], in_=S32[:]), reads=[b_S], writes=[bnsb])
                    cur_sbf = (nsb, bnsb)
                    if GLA_LEVEL < 8:
                        continue
                    if d == 0 and grp == 0 and blk == 1:
                        dbg_dump(c, "lt", lt[:], bl, [128, 256], F32)
                        dbg_dump(c, "bsb", bsb[:], bbsb, [128, 2, 128], F32)
                        dbg_dump(c, "col", col[:], bcol, [128, 8], F32)
                        dbg_dump(c, "Et", Et[:], bE, [128, 4, 2, 128], F32)
                        dbg_dump(c, "qk", qk[:], bqk, [128, 4, 2, 128], BF16)
                        dbg_dump(c, "qz", qz[:], bqz, [128, 2, 2, 2, 128], BF16)
                        dbg_dump(c, "klT", klT[:], bklT, [128, 256], BF16)
                        dbg_dump(c, "sTm", sTm[:], bsT, [128, 512], BF16)
                        dbg_dump(c, "S32", S32[:], b_S, [128, 2, 128], F32)
                    if d == 0:
                        oc, boc = ocs.next()
                        P.op("scalar", lambda e, oc=oc, ps5=ps5: e.copy(out=oc[:], in_=ps5[:, :]), reads=[bps5], writes=[boc])
                        P.dma("sync", ofv[:, :, tb:tb + 128], oc[:].rearrange("p (h t) -> p h t", h=4), reads=[boc], mwrites=[B["ofT"]])
                    else:
                        of, bof = ofs.next()
                        P.dma("sync", of[:].rearrange("p (h t) -> p h t", h=4), ofv[:, :, tb:tb + 128], reads=[B["ofT"]], writes=[bof])
                        oc, boc = ocs.next()
                        P.op("vector", lambda e, oc=oc, ps5=ps5, of=of: e.tensor_tensor(out=oc[:], in0=ps5[:, :], in1=of[:], op=ALU.add), reads=[bps5, bof], writes=[boc])
                        sq, bsq = sqs.next()
                        P.op("scalar", lambda e, sq=sq, oc=oc: e.activation(out=sq[:], in_=oc[:], func=AF.Square), reads=[boc], writes=[bsq])
                        ps7, bps7 = c.psum.next()
                        P.op("tensor", lambda e, ps7=ps7, sq=sq: e.matmul(ps7[:, :], c.ones_b[:], sq[:], start=True, stop=True), reads=[bsq, c.b_const], writes=[bps7])
                        rs, brs = rss.next()
                        P.op("scalar", lambda e, rs=rs, ps7=ps7: e.activation(out=rs[:], in_=ps7[:, :], func=AF.Sqrt, bias=c.eps_col[:, 0:1], scale=1.0 / 128),
                             reads=[bps7, c.b_const], writes=[brs])
                        P.op("vector", lambda e, rs=rs: e.reciprocal(out=rs[:], in_=rs[:]), reads=[brs], writes=[brs])
                        sg, bsg = sgs.next()
                        P.op("scalar", lambda e, sg=sg, gt=gt, bs=bs: e.activation(out=sg[:].rearrange("p (h t) -> p h t", h=4), in_=gt[:, :, bs], func=AF.Silu),
                             reads=[bg], writes=[bsg])
                        P.op("vector", lambda e, oc=oc, rs=rs: e.tensor_tensor(out=oc[:], in0=oc[:], in1=rs[:], op=ALU.mult), reads=[boc, brs], writes=[boc])
                        st, bst = c.stage.next()
                        sv = st[:].bitcast(BF16)[:, :512]
                        P.op("vector", lambda e, sv=sv, oc=oc, sg=sg: e.scalar_tensor_tensor(out=sv, in0=oc[:], scalar=c.pp[:, l, PP_GLAG:PP_GLAG + 1], in1=sg[:],
                                                                                          op0=ALU.mult, op1=ALU.mult), reads=[boc, bsg, c.b_const], writes=[bst])
                        P.dma("sync", mxv[:, :, tb:tb + 128], sv.rearrange("p (h t) -> p h t", h=4), reads=[bst], mwrites=[B["mixT"]])
    P.barrier()


def build_program(layers=(0, 1, 2, 3), final=True, dbg_out=()):
    nc = bass.Bass("TRN2", target_bir_lowering=False)
    T, B = declare(nc, dbg_out=dbg_out)
    declare_consts(nc, T, B)
    declare_hy(nc, T, B)
    declare_gla(nc, T, B)
    T["xresB"] = nc.dram_tensor("xresB", [D, L], F32, kind="Internal").ap()
    B["xresB"] = Buf("xresB")
    T["yT"] = nc.dram_tensor("yT", [D, L], F32, kind="ExternalOutput").ap()
    B["yT"] = Buf("yT")
    c = Ctx(nc)
    load_params(c, T, B)
    setup_ident(c, T, B)
    setup_mem(c, T, B)
    cur = "xT"
    for l in layers:
        phase_inproj(c, T, B, l, cur)
        phase_gla(c, T, B, l)
        phase_fnet(c, T, B, l)
        phase_hyena(c, T, B, l)
        phase_shortconv(c, T, B, l)
        a, b = ("xres", "xresB") if cur != "xres" else ("xresB", "xres")
        phase_outproj(c, T, B, l, cur, a)
        phase_xattn(c, T, B, l, a, b)
        phase_ffn(c, T, B, l, b, a)
        cur = a
    if final:
        rmsnorm_fm(c, T[cur], B[cur], 0, L, c.pg[:, 16:32], None, None, out_dram=(T["yT"], B["yT"]))
    else:
        for k in range(KC):
            st, bst = c.stage.next()
            for tt in range(L // 512):
                st, bst = c.stage.next()
                c.P.dma("sync", st[:], T[cur][k * 128:(k + 1) * 128, tt * 512:(tt + 1) * 512], reads=[B[cur]], writes=[bst])
                c.P.dma("sync", T["yT"][k * 128:(k + 1) * 128, tt * 512:(tt + 1) * 512], st[:], reads=[bst], mwrites=[B["yT"]], store=True)
    c.P.finish()
    return nc, c


def pack_small(inp):
    g = lambda k: np.asarray(inp[k], dtype=np.float32)
    pp = np.zeros((DEPTH, 128, NPP), np.float32)
    for l in range(DEPTH):
        for j in range(3):
            pp[l, :, PP_NG + 16 * j: PP_NG + 16 * (j + 1)] = g("norm_g")[l, j].reshape(16, 128).T
        pp[l, :, PP_GLAG] = g("gla_norm_g")[l]
        pp[l, :, PP_HCW:PP_HCW + 36] = g("hy_conv_w")[l].reshape(3, 12, 128).transpose(2, 1, 0).reshape(128, 36)
        pp[l, :, PP_HSKIP:PP_HSKIP + 8] = g("hy_skip")[l].reshape(2, 4, 128).transpose(2, 0, 1).reshape(128, 8)
        pp[l, :, PP_SCW:PP_SCW + 12] = g("sc_conv_w")[l].reshape(3, 4, 128).transpose(2, 1, 0).reshape(128, 12)
        pp[l, :, PP_GRPG:PP_GRPG + 12] = g("grp_norm_g")[l].reshape(3, 4, 128).transpose(2, 0, 1).reshape(128, 12)
        pp[l, :64, PP_HB1] = g("hy_ffn_b1")[l]
        pp[l, :64, PP_HB2] = g("hy_ffn_b2")[l]
        pp[l, :64, PP_HFREQ] = g("hy_sin_freq")[l]
    pg = np.zeros((128, 32), np.float32)
    pg[:, 0:16] = g("mem_norm_g").reshape(16, 128).T
    pg[:, 16:32] = g("final_norm_g").reshape(16, 128).T
    gkw = np.zeros((DEPTH, 2, 32, 256), np.float32)
    gkw[:, 0, :16] = g("gla_gk_w")[:, 0]
    gkw[:, 1, 16:] = g("gla_gk_w")[:, 1]
    return {"pp": pp, "pg": pg, "gkw": gkw, "gkb": np.ascontiguousarray(g("gla_gk_b")),
            "hw1": np.ascontiguousarray(g("hy_ffn_w1")), "hw2": np.ascontiguousarray(g("hy_ffn_w2")),
            "hw3": np.ascontiguousarray(g("hy_ffn_w3")), "hdec": np.ascontiguousarray(g("hy_decay").reshape(DEPTH, 2048))}


_CONSTS = None


def all_consts():
    global _CONSTS
    if _CONSTS is None:
        cst = {}
        cst.update(host_consts())
        cst.update(hy_consts())
        cst.update(gla_consts())
        _CONSTS = cst
    return _CONSTS


def make_in_maps(inputs, ncores=8):
    shared = {}
    for n, _ in WEIGHTS:
        shared[n] = np.ascontiguousarray(np.asarray(inputs[n], dtype=np.float32))
    shared.update(pack_small(inputs))
    shared.update(all_consts())
    x = np.asarray(inputs["x"], dtype=np.float32)
    mem = np.asarray(inputs["mem"], dtype=np.float32)
    maps = []
    for b in range(ncores):
        m = dict(shared)
        m["xT"] = np.ascontiguousarray(x[b].T)
        m["memT"] = np.ascontiguousarray(mem[b].T)
        maps.append(m)
    return maps


def kernel(**inputs):
    nc, c = build_program()
    maps = make_in_maps(inputs, 8)
    res = run_bass_kernel_spmd(nc, maps, core_ids=list(range(8)))
    out = np.empty((8, L, D), np.float32)
    for b in range(8):
        out[b] = np.asarray(res.results[b]["yT"]).T
    return out
```

```python
import math
import numpy as np
import ml_dtypes
import concourse.bass as bass
import concourse.mybir as mybir
from concourse.bass_utils import run_bass_kernel_spmd

F32 = mybir.dt.float32
BF16 = mybir.dt.bfloat16
AF = mybir.ActivationFunctionType
ALU = mybir.AluOpType

D = 2048
L = 4096
DEPTH = 4
NMEM = 256
D_IN = 5152
D_FF = 5632
EPS = 1e-6
KC = D // 128

ENGS = ("sync", "scalar", "vector", "gpsimd", "tensor")
DMA_K = 8
import os
STORE_Q = os.environ.get("STORE_Q", "gpsimd")
if STORE_Q == "none":
    STORE_Q = None


class Buf:
    __slots__ = ("name", "w", "rs", "rd", "wc", "wd")

    def __init__(self, name=""):
        self.name = name
        self.w = None
        self.wc = {}
        self.wd = []
        self.rs = {}
        self.rd = []


class Op:
    __slots__ = ("eng", "fn", "deps", "sig", "sem", "tick", "is_dma")

    def __init__(self, eng, fn, is_dma=False):
        self.eng = eng
        self.fn = fn
        self.deps = set()
        self.sig = False
        self.sem = None
        self.tick = 0
        self.is_dma = is_dma


class Prog:
    def __init__(self, nc):
        self.nc = nc
        self.ops = {e: [] for e in ENGS}
        self.esem = {e: nc.alloc_semaphore("se_" + e) for e in ("scalar", "vector", "gpsimd", "tensor")}
        self.dsem = {}
        self.dn = {}
        self.dlast = {}
        self.dcnt = {}
        self.stores = []
        self.last_real = {}
        self.live_dma = []
        self.store_q = None

    def _track(self, o, reads, writes, mwrites=()):
        deps = o.deps
        for b in reads:
            if b.w is not None:
                deps.add(b.w)
            deps.update(b.wc.values())
            deps.update(b.wd)
        for b in writes:
            if b.w is not None:
                deps.add(b.w)
            deps.update(b.wc.values())
            deps.update(b.wd)
            deps.update(b.rs.values())
            deps.update(b.rd)
        for b in mwrites:
            if b.rs or b.rd:
                deps.update(b.rs.values())
                deps.update(b.rd)
                b.rs = {}
                b.rd = []
                b.wc = {}
                b.wd = []
                b.w = None
            if b.w is not None:
                deps.add(b.w)
        for b in reads:
            if o.is_dma:
                b.rd.append(o)
            else:
                b.rs[o.eng] = o
        for b in writes:
            b.w = o
            b.wc = {}
            b.wd = []
            b.rs = {}
            b.rd = []
        for b in mwrites:
            if o.is_dma:
                b.wd.append(o)
            else:
                b.wc[o.eng] = o
        deps.discard(o)

    def op(self, eng, fn, reads=(), writes=(), mwrites=(), extra=()):
        o = Op(eng, fn)
        o.sem = self.esem[eng]
        self._track(o, reads, writes, mwrites)
        o.deps.update(extra)
        self.ops[eng].append(o)
        self.last_real[eng] = o
        return o

    def dma(self, q, out, in_, reads=(), writes=(), mwrites=(), store=False):
        if self.store_q is not None and type(out.tensor).__name__ == "DRamTensorHandle":
            q = self.store_q

        def fn(eng, out=out, in_=in_):
            return eng.dma_start(out=out, in_=in_)
        o = Op(q, fn, is_dma=True)
        self._track(o, reads, writes, mwrites)
        if q not in self.dsem:
            self.dsem[q] = [self.nc.alloc_semaphore("sd_%s%d" % (q, i)) for i in range(DMA_K)]
            self.dn[q] = 0
            self.dlast[q] = [None] * DMA_K
            self.dcnt[q] = [0] * DMA_K
        slot = self.dn[q] % DMA_K
        self.dn[q] += 1
        prev = self.dlast[q][slot]
        if prev is not None:
            o.deps.add(prev)
        self.dlast[q][slot] = o
        self.dcnt[q][slot] += 1
        o.sem = self.dsem[q][slot]
        o.tick = 16 * self.dcnt[q][slot]
        o.sig = True
        self.ops[q].append(o)
        if store:
            self.stores.append(o)
        return o

    def barrier(self):
        deps = set(self.last_real.values())
        for q in self.dlast:
            for p in self.dlast[q]:
                if p is not None:
                    deps.add(p)
        for e in ENGS:
            if not self.ops[e]:
                continue
            o = Op(e, lambda eng: None)
            o.deps.update(deps)
            self.ops[e].append(o)

    def finish(self):
        o = Op("sync", lambda eng: None)
        o.deps.update(self.stores)
        for q in self.dlast:
            for p in self.dlast[q]:
                if p is not None:
                    o.deps.add(p)
        self.ops["sync"].append(o)
        for e in ENGS:
            for o in self.ops[e]:
                for d in o.deps:
                    d.sig = True
        for e in ("scalar", "vector", "gpsimd", "tensor"):
            c = 0
            for o in self.ops[e]:
                if o.is_dma:
                    continue
                if o.sig:
                    c += 1
                    o.tick = c
        nc = self.nc
        with nc.Block() as block:
            for e in ENGS:
                if not self.ops[e]:
                    continue

                def body(eng, e=e):
                    waited = {}
                    for o in self.ops[e]:
                        need = {}
                        for d in o.deps:
                            k = d.sem.num
                            if need.get(k, (None, 0))[1] < d.tick:
                                need[k] = (d.sem, d.tick)
                        for k, (sem, v) in need.items():
                            if waited.get(k, 0) < v:
                                eng.wait_ge(sem, v)
                                waited[k] = v
                        ins = o.fn(eng)
                        if ins is not None and o.sig:
                            ins.then_inc(o.sem, 16 if o.is_dma else 1)

                getattr(block, e)(body)


_UID = [0]


def sbt(nc, name, shape, dt):
    _UID[0] += 1
    return nc.sbuf_tensor("%s_%d" % (name, _UID[0]), shape, dt)


class Pool:
    def __init__(self, items):
        self.items = items
        self.i = 0

    def next(self):
        it = self.items[self.i % len(self.items)]
        self.i += 1
        return it


WELEMS = 5760
NW = 4
NST = 4


class Ctx:
    def __init__(self, nc):
        self.nc = nc
        self.P = Prog(nc)
        self.psum = Pool([(nc.alloc_psum_tensor("ps%d" % i, [128, 512], F32), Buf("ps%d" % i)) for i in range(8)])
        self.wpool = Pool([(nc.alloc_sbuf_tensor("wb%d" % i, [128, WELEMS], BF16), Buf("wb%d" % i)) for i in range(NW)])
        self.stage = Pool([(nc.alloc_sbuf_tensor("stg%d" % i, [128, 512], F32), Buf("stg%d" % i)) for i in range(NST)])
        self.ones_b = nc.alloc_sbuf_tensor("ones_b", [128, 128], BF16)
        self.b_const = Buf("const")
        self.P.op("vector", lambda e: e.memset(self.ones_b[:], 1.0), mwrites=[self.b_const])
        self.eps_col = nc.alloc_sbuf_tensor("eps_col", [128, 1], F32)
        self.P.op("vector", lambda e: e.memset(self.eps_col[:], EPS), mwrites=[self.b_const])
        self.rpool = Pool([(nc.alloc_sbuf_tensor("rtile%d" % i, [128, 512], F32), Buf("rt%d" % i)) for i in range(4)])
        self.evac_i = 0

    def evac_eng(self):
        self.evac_i += 1
        return "vector" if self.evac_i % 2 else "scalar"


def copy_op(c, eng, out, in_, reads, writes):
    if eng == "scalar":
        return c.P.op("scalar", lambda e: e.copy(out=out, in_=in_), reads=reads, writes=writes)
    return c.P.op(eng, lambda e: e.tensor_copy(out=out, in_=in_), reads=reads, writes=writes)


def load_wtile(c, Wv, kcn, c0, mw):
    wt, bw = c.wpool.next()
    view = wt[:, :kcn * mw].rearrange("p (k m) -> p k m", m=mw)
    c.P.dma("gpsimd", view, Wv[:, :, c0:c0 + mw], writes=[bw])
    return view, bw


def linear_fm(c, act, b_act, kcn, TB, W, cols, epi, mw_tile=None, pre=None, tw=512):
    P = c.P
    Wv = W.rearrange("(k p) m -> p k m", p=128)
    c0, n = cols
    if mw_tile is None:
        mw_tile = 256 if kcn <= 22 else 128
    tw = min(tw, TB)
    steps = []
    off = 0
    while off < n:
        mw = min(mw_tile, n - off)
        first = True
        for mj in range(0, mw, 128):
            w = min(128, mw - mj)
            for tt in range(TB // tw):
                steps.append((off, mw, mj, w, tt, first))
                first = False
        off += mw
    toks = {}
    if pre is not None and steps:
        o, mw, mj, w, tt, f = steps[0]
        toks[0] = pre(o + mj, w, tt)
    wv = bw = None
    for i, (o, mw, mj, w, tt, f) in enumerate(steps):
        if f:
            wv, bw = load_wtile(c, Wv, kcn, c0 + o, mw)
        if pre is not None and i + 1 < len(steps):
            o2, mw2, mj2, w2, tt2, f2 = steps[i + 1]
            toks[i + 1] = pre(o2 + mj2, w2, tt2)
        ps, bps = c.psum.next()

        def mm(e, wv=wv, mj=mj, w=w, tt=tt, ps=ps):
            for k in range(kcn):
                ins = e.matmul(ps[:w, :tw], wv[:, k, mj:mj + w], act[:, k, tt * tw:(tt + 1) * tw],
                               start=(k == 0), stop=(k == kcn - 1))
            return ins
        P.op("tensor", mm, reads=[bw, b_act], writes=[bps])
        epi(o + mj, w, tt, ps[:w, :tw], bps, toks.pop(i, None))


def linear_tm(c, act, b_act, kcn, TB, W, cols, epi):
    P = c.P
    Wv = W.rearrange("(k p) m -> p k m", p=128)
    c0, n = cols
    off = 0
    while off < n:
        mw = min(256, n - off)
        wv, bw = load_wtile(c, Wv, kcn, c0 + off, mw)
        for tc in range(TB // 128):
            ps, bps = c.psum.next()

            def mm(e, wv=wv, tc=tc, ps=ps, mw=mw):
                for k in range(kcn):
                    ins = e.matmul(ps[:, :mw], act[:, k, tc * 128:(tc + 1) * 128], wv[:, k, :],
                                   start=(k == 0), stop=(k == kcn - 1))
                return ins
            P.op("tensor", mm, reads=[bw, b_act], writes=[bps])
            epi(off, mw, tc, ps[:, :mw], bps)
        off += mw


def rmsnorm_fm(c, src, b_src, tok0, TB, gcols, hn, b_hn, NT=128, out_dram=None):
    nc, P = c.nc, c.P
    srcv = src.rearrange("(k p) t -> p k t", p=128)
    with sbt(nc, "nx0", [128, KC, NT], F32) as x0, sbt(nc, "nx1", [128, KC, NT], F32) as x1, \
            sbt(nc, "nsq0", [128, KC, NT], BF16) as s0, sbt(nc, "nsq1", [128, KC, NT], BF16) as s1, \
            sbt(nc, "nrs0", [128, NT], F32) as r0, sbt(nc, "nrs1", [128, NT], F32) as r1:
        xs = Pool([(x0, Buf()), (x1, Buf())])
        ss = Pool([(s0, Buf()), (s1, Buf())])
        rs = Pool([(r0, Buf()), (r1, Buf())])
        for i in range(TB // NT):
            xt, bx = xs.next()
            sq, bs = ss.next()
            rt, br = rs.next()
            t0 = tok0 + i * NT
            P.dma("sync", xt[:], srcv[:, :, t0:t0 + NT], reads=[b_src], writes=[bx])
            P.op("scalar", lambda e, sq=sq, xt=xt: e.activation(out=sq[:], in_=xt[:], func=AF.Square),
                 reads=[bx], writes=[bs])
            ps, bps = c.psum.next()

            def mm(e, sq=sq, ps=ps):
                for k in range(KC):
                    ins = e.matmul(ps[:, :NT], c.ones_b[:], sq[:, k, :], start=(k == 0), stop=(k == KC - 1))
                return ins
            P.op("tensor", mm, reads=[bs, c.b_const], writes=[bps])
            P.op("scalar", lambda e, rt=rt, ps=ps: e.activation(out=rt[:], in_=ps[:, :NT], func=AF.Sqrt, bias=c.eps_col[:, 0:1],
                                                                scale=1.0 / D), reads=[bps, c.b_const], writes=[br])
            P.op("vector", lambda e, rt=rt: e.reciprocal(out=rt[:], in_=rt[:]), reads=[br], writes=[br])
            if out_dram is not None:
                dst, bdst = out_dram
                dstv = dst.rearrange("(k p) t -> p k t", p=128)
                for k in range(KC):
                    P.op("vector", lambda e, k=k, xt=xt, rt=rt: e.scalar_tensor_tensor(
                        out=xt[:, k, :], in0=xt[:, k, :], scalar=gcols[:, k:k + 1], in1=rt[:],
                        op0=ALU.mult, op1=ALU.mult), reads=[bx, br, c.b_const], writes=[bx])
                P.dma("sync", dstv[:, :, t0:t0 + NT], xt[:], reads=[bx], mwrites=[bdst], store=True)
                continue
            for k in range(KC):
                P.op("vector", lambda e, k=k, xt=xt, rt=rt, i=i: e.scalar_tensor_tensor(
                    out=hn[:, k, i * NT:(i + 1) * NT], in0=xt[:, k, :], scalar=gcols[:, k:k + 1], in1=rt[:],
                    op0=ALU.mult, op1=ALU.mult), reads=[bx, br, c.b_const], mwrites=[b_hn])
    P.barrier()


NPP = 120
PP_NG = 0
PP_GLAG = 48
PP_HCW = 49
PP_HSKIP = 85
PP_SCW = 93
PP_GRPG = 105
PP_HB1 = 117
PP_HB2 = 118
PP_HFREQ = 119

SEGS = [
    ("qT", 0, 256, "fm32"), ("kT", 256, 256, "fm32"), ("vtok", 512, 512, "tm16"), ("gT", 1024, 512, "fm32"),
    ("lrT", 1536, 32, "fm32"), ("ufT", 1568, 512, "fm16"), ("uhT", 2080, 1536, "fm32"), ("usT", 3616, 1536, "fm32"),
]

WEIGHTS = [("w_in", [DEPTH, D, D_IN]), ("w_out", [DEPTH, D, D]), ("w_xq", [DEPTH, D, D]), ("w_xkv", [DEPTH, D, 2 * D]),
           ("w_xo", [DEPTH, D, D]), ("w_gate_up", [DEPTH, D, 2 * D_FF]), ("w_down", [DEPTH, D_FF, D])]
SMALL = [("pp", [DEPTH, 128, NPP]), ("pg", [128, 32]), ("gkw", [DEPTH, 2, 32, 256]), ("gkb", [DEPTH, 2, 256]),
         ("hw1", [DEPTH, 33, 64]), ("hw2", [DEPTH, 64, 64]), ("hw3", [DEPTH, 64, 2048]), ("hdec", [DEPTH, 2048])]
SCRATCH = [("xres", [D, L], F32), ("qT", [256, L], F32), ("kT", [256, L], F32), ("vtok", [L, 512], BF16),
           ("gT", [512, L], F32), ("lrT", [32, L], F32), ("ufT", [512, L], BF16), ("uhT", [1536, L], F32),
           ("usT", [1536, L], F32), ("mixT", [D, L], BF16)]


def declare(nc, dbg_out=(), dbg_in=()):
    T = {}
    B = {}
    T["xT"] = nc.dram_tensor("xT", [D, L], F32, kind="ExternalInput").ap()
    T["memT"] = nc.dram_tensor("memT", [D, NMEM], F32, kind="ExternalInput").ap()
    for n, shp in WEIGHTS + SMALL:
        T[n] = nc.dram_tensor(n, shp, F32, kind="ExternalInput").ap()
    for n, shp, dt in SCRATCH:
        kind = "ExternalOutput" if n in dbg_out else ("ExternalInput" if n in dbg_in else "Internal")
        T[n] = nc.dram_tensor(n, shp, dt, kind=kind).ap()
    for n in T:
        B[n] = Buf(n)
    return T, B


def load_params(c, T, B):
    nc, P = c.nc, c.P
    c.pp = nc.alloc_sbuf_tensor("pp_sb", [128, DEPTH, NPP], F32)
    c.pg = nc.alloc_sbuf_tensor("pg_sb", [128, 32], F32)
    for l in range(DEPTH):
        P.dma("sync", c.pp[:, l, :], T["pp"][l], mwrites=[c.b_const])
    P.dma("sync", c.pg[:], T["pg"], mwrites=[c.b_const])


TB = 1024


def phase_inproj(c, T, B, l, xname):
    nc, P = c.nc, c.P
    W = T["w_in"][l]
    for blk in range(L // TB):
        tok0 = blk * TB
        with sbt(nc, "hn", [128, KC, TB], BF16) as hn:
            b_hn = Buf("hn")
            rmsnorm_fm(c, T[xname], B[xname], tok0, TB, c.pp[:, l, PP_NG:PP_NG + 16], hn, b_hn)
            for name, c0, n, kind in SEGS:
                dst, bd = T[name], B[name]
                if kind == "tm16":
                    def epi(off, mw, tc, ps, bps, dst=dst, bd=bd):
                        st, bst = c.stage.next()
                        sv = st[:].bitcast(BF16)[:, :mw]
                        copy_op(c, c.evac_eng(), sv, ps, [bps], [bst])
                        r0 = tok0 + tc * 128
                        P.dma("sync", dst[r0:r0 + 128, off:off + mw], sv, reads=[bst], mwrites=[bd])
                    linear_tm(c, hn, b_hn, KC, TB, W, (c0, n), epi)
                else:
                    def epi(ci, w, tt, ps, bps, tok, dst=dst, bd=bd, kind=kind):
                        st, bst = c.stage.next()
                        sv = st[:w, :] if kind == "fm32" else st[:].bitcast(BF16)[:w, :512]
                        copy_op(c, c.evac_eng(), sv, ps, [bps], [bst])
                        t0 = tok0 + tt * 512
                        P.dma("sync", dst[ci:ci + w, t0:t0 + 512], sv, reads=[bst], mwrites=[bd])
                    linear_fm(c, hn, b_hn, KC, TB, W, (c0, n), epi)
            P.barrier()


def make_resid(c, T, B, xin, xout, tok0):
    P, nc = c.P, c.nc

    def pre(ci, w, tt):
        xt, bx = c.rpool.next()
        t0 = tok0 + tt * 512
        P.dma("sync", xt[:w, :], T[xin][ci:ci + w, t0:t0 + 512], reads=[B[xin]], writes=[bx])
        return xt, bx

    def epi(ci, w, tt, ps, bps, tok):
        xt, bx = tok
        t0 = tok0 + tt * 512
        P.op("vector", lambda e: e.tensor_tensor(out=xt[:w, :], in0=ps, in1=xt[:w, :], op=ALU.add), reads=[bps, bx], writes=[bx])
        P.dma("sync", T[xout][ci:ci + w, t0:t0 + 512], xt[:w, :], reads=[bx], mwrites=[B[xout]])
    return pre, epi


def phase_outproj(c, T, B, l, xin, xout):
    nc, P = c.nc, c.P
    mv = T["mixT"].rearrange("(k p) t -> p k t", p=128)
    for blk in range(L // TB):
        tok0 = blk * TB
        with sbt(nc, "mix", [128, KC, TB], BF16) as mix:
            b_mix = Buf("mix")
            for k in range(0, KC, 4):
                P.dma("sync", mix[:, k:k + 4, :], mv[:, k:k + 4, tok0:tok0 + TB], reads=[B["mixT"]], mwrites=[b_mix])
            pre, epi = make_resid(c, T, B, xin, xout, tok0)
            linear_fm(c, mix, b_mix, KC, TB, T["w_out"][l], (0, D), epi, pre=pre)
            P.barrier()


def phase_ffn(c, T, B, l, xin, xout):
    nc, P = c.nc, c.P
    FC = D_FF // 128
    Wgu = T["w_gate_up"][l]
    Wv = Wgu.rearrange("(k p) m -> p k m", p=128)
    for blk in range(L // TB):
        tok0 = blk * TB
        with sbt(nc, "h3", [128, KC, TB], BF16) as h3:
            b_h3 = Buf("h3")
            rmsnorm_fm(c, T[xin], B[xin], tok0, TB, c.pp[:, l, PP_NG + 32:PP_NG + 48], h3, b_h3)
            with sbt(nc, "aT", [128, FC, TB], BF16) as aT:
                b_aT = Buf("aT")
                for j0 in range(0, FC, 2):
                    wg, bwg = load_wtile(c, Wv, KC, j0 * 128, 256)
                    wu, bwu = load_wtile(c, Wv, KC, D_FF + j0 * 128, 256)
                    for mj in range(2):
                        j = j0 + mj
                        for tt in range(TB // 512):
                            psg, bpg = c.psum.next()
                            psu, bpu = c.psum.next()

                            def mm(e, wt, ps, mj=mj, tt=tt):
                                for k in range(KC):
                                    ins = e.matmul(ps[:, :], wt[:, k, mj * 128:(mj + 1) * 128], h3[:, k, tt * 512:(tt + 1) * 512],
                                                   start=(k == 0), stop=(k == KC - 1))
                                return ins
                            P.op("tensor", lambda e, wg=wg, psg=psg, mm=mm: mm(e, wg, psg), reads=[bwg, b_h3], writes=[bpg])
                            P.op("tensor", lambda e, wu=wu, psu=psu, mm=mm: mm(e, wu, psu), reads=[bwu, b_h3], writes=[bpu])
                            st, bst = c.stage.next()
                            P.op("scalar", lambda e, st=st, psg=psg: e.activation(out=st[:], in_=psg[:], func=AF.Silu),
                                 reads=[bpg], writes=[bst])
                            P.op("vector", lambda e, st=st, psu=psu, j=j, tt=tt: e.tensor_tensor(
                                out=aT[:, j, tt * 512:(tt + 1) * 512], in0=psu[:], in1=st[:], op=ALU.mult),
                                reads=[bpu, bst], mwrites=[b_aT])
                pre, epi = make_resid(c, T, B, xin, xout, tok0)
                linear_fm(c, aT, b_aT, FC, TB, T["w_down"][l], (0, D), epi, pre=pre)
                P.barrier()


def setup_mem(c, T, B):
    nc = c.nc
    c.memn = nc.alloc_sbuf_tensor("memn", [128, KC, NMEM], BF16)
    c.b_memn = Buf("memn")
    rmsnorm_fm(c, T["memT"], B["memT"], 0, NMEM, c.pg[:, 0:16], c.memn, c.b_memn)


def phase_xattn(c, T, B, l, xin, xout):
    nc, P = c.nc, c.P
    with sbt(nc, "kTm", [128, KC, NMEM], BF16) as kTm, sbt(nc, "Vm", [128, 2, D], BF16) as Vm, \
            sbt(nc, "pT0", [128, 512], BF16) as pT0, sbt(nc, "pT1", [128, 512], BF16) as pT1, \
            sbt(nc, "pT2", [128, 512], BF16) as pT2, sbt(nc, "pT3", [128, 512], BF16) as pT3, \
            sbt(nc, "rden0", [128, 512], F32) as rd0, sbt(nc, "rden1", [128, 512], F32) as rd1:
        c.kTm, c.Vm = kTm, Vm
        c.b_kTm, c.b_Vm = Buf("kTm"), Buf("Vm")
        c.pT = Pool([(t, Buf()) for t in (pT0, pT1, pT2, pT3)])
        c.rden = Pool([(t, Buf()) for t in (rd0, rd1)])
        _phase_xattn(c, T, B, l, xin, xout)


def _phase_xattn(c, T, B, l, xin, xout):
    nc, P = c.nc, c.P
    Wkv = T["w_xkv"][l]

    def epi_k(ci, w, tt, ps, bps, tok):
        eng = c.evac_eng()
        if eng == "scalar":
            P.op("scalar", lambda e: e.copy(out=c.kTm[:w, ci // 128, :], in_=ps), reads=[bps], mwrites=[c.b_kTm])
        else:
            P.op("vector", lambda e: e.tensor_copy(out=c.kTm[:w, ci // 128, :], in_=ps), reads=[bps], mwrites=[c.b_kTm])
    linear_fm(c, c.memn, c.b_memn, KC, NMEM, Wkv, (0, D), epi_k, tw=NMEM)

    def epi_v(off, mw, tc, ps, bps):
        eng = c.evac_eng()
        if eng == "scalar":
            P.op("scalar", lambda e: e.copy(out=c.Vm[:, tc, off:off + mw], in_=ps), reads=[bps], mwrites=[c.b_Vm])
        else:
            P.op("vector", lambda e: e.tensor_copy(out=c.Vm[:, tc, off:off + mw], in_=ps), reads=[bps], mwrites=[c.b_Vm])
    linear_tm(c, c.memn, c.b_memn, KC, NMEM, Wkv, (D, D), epi_v)

    scale = 512.0 ** -0.5
    for blk in range(L // TB):
        tok0 = blk * TB
        with sbt(nc, "qTs", [128, KC, TB], BF16) as qTs:
            b_at, b_q = Buf("attnT"), Buf("qTs")
            with sbt(nc, "hq", [128, KC, TB], BF16) as hq:
                b_hq = Buf("hq")
                rmsnorm_fm(c, T[xin], B[xin], tok0, TB, c.pp[:, l, PP_NG + 16:PP_NG + 32], hq, b_hq)

                def epi_q(ci, w, tt, ps, bps, tok):
                    eng = c.evac_eng()
                    dst = qTs[:w, ci // 128, tt * 512:(tt + 1) * 512]
                    if eng == "scalar":
                        P.op("scalar", lambda e: e.mul(out=dst, in_=ps, mul=scale), reads=[bps], mwrites=[b_q])
                    else:
                        P.op("vector", lambda e: e.tensor_scalar_mul(out=dst, in0=ps, scalar1=scale), reads=[bps], mwrites=[b_q])
                linear_fm(c, hq, b_hq, KC, TB, T["w_xq"][l], (0, D), epi_q)
                P.barrier()
            _xattn_core(c, T, B, l, xin, xout, tok0, qTs, b_q)


def _xattn_core(c, T, B, l, xin, xout, tok0, qTs, b_q):
    nc, P = c.nc, c.P
    if True:
        with sbt(nc, "attnT", [128, KC, TB], BF16) as attnT:
            b_at = Buf("attnT")
            for tt in range(TB // 512):
                ts = slice(tt * 512, (tt + 1) * 512)
                for h in range(4):
                    pts = []
                    for mc in range(2):
                        ps, bps = c.psum.next()

                        def mm(e, ps=ps, mc=mc, h=h, ts=ts):
                            for dc in range(4):
                                ins = e.matmul(ps[:, :], c.kTm[:, h * 4 + dc, mc * 128:(mc + 1) * 128], qTs[:, h * 4 + dc, ts],
                                               start=(dc == 0), stop=(dc == 3))
                            return ins
                        P.op("tensor", mm, reads=[c.b_kTm, b_q], writes=[bps])
                        pt, bpt = c.pT.next()
                        P.op("scalar", lambda e, pt=pt, ps=ps: e.activation(out=pt[:], in_=ps[:], func=AF.Exp), reads=[bps], writes=[bpt])
                        pts.append((pt, bpt))
                    psd, bpd = c.psum.next()

                    def mmd(e, psd=psd, pts=pts):
                        for mc in range(2):
                            ins = e.matmul(psd[:, :], c.ones_b[:], pts[mc][0][:], start=(mc == 0), stop=(mc == 1))
                        return ins
                    P.op("tensor", mmd, reads=[pts[0][1], pts[1][1], c.b_const], writes=[bpd])
                    rd, brd = c.rden.next()
                    P.op("vector", lambda e, rd=rd, psd=psd: e.reciprocal(out=rd[:], in_=psd[:]), reads=[bpd], writes=[brd])
                    for dc in range(4):
                        pso, bpo = c.psum.next()

                        def mmo(e, pso=pso, pts=pts, h=h, dc=dc):
                            for mc in range(2):
                                ins = e.matmul(pso[:, :], c.Vm[:, mc, (h * 4 + dc) * 128:(h * 4 + dc + 1) * 128], pts[mc][0][:],
                                               start=(mc == 0), stop=(mc == 1))
                            return ins
                        P.op("tensor", mmo, reads=[c.b_Vm, pts[0][1], pts[1][1]], writes=[bpo])
                        P.op("vector", lambda e, pso=pso, rd=rd, h=h, dc=dc, ts=ts: e.tensor_tensor(
                            out=attnT[:, h * 4 + dc, ts], in0=pso[:], in1=rd[:], op=ALU.mult), reads=[bpo, brd], mwrites=[b_at])
            pre, epi = make_resid(c, T, B, xin, xout, tok0)
            linear_fm(c, attnT, b_at, KC, TB, T["w_xo"][l], (0, D), epi, pre=pre)
            P.barrier()


GN = 4224
NF = 8192


def host_consts():
    a = np.arange(GN, dtype=np.int64)
    m = (a[:, None] * a[None, :]) % NF
    ang = m.astype(np.float64) * (2.0 * np.pi / NF)
    Gc = np.cos(ang).astype(ml_dtypes.bfloat16)
    Gs = np.sin(ang).astype(ml_dtypes.bfloat16)
    ch = np.arange(128, dtype=np.int64)
    angc = ((ch[:, None] * ch[None, :]) % 128).astype(np.float64) * (2.0 * np.pi / 128)
    sc = 1.0 / math.sqrt(4096.0 * 128.0)
    csc = np.concatenate([np.cos(angc) * sc, -np.sin(angc) * sc], axis=1).astype(np.float32)
    sgn = np.tile(np.array([1.0, -1.0], np.float32), 256)[None, :].repeat(128, 0)
    return {"Gc": Gc, "Gs": Gs, "csc": csc, "sgn": np.ascontiguousarray(sgn)}


def declare_consts(nc, T, B):
    T["Gc"] = nc.dram_tensor("Gc", [GN, GN], BF16, kind="ExternalInput").ap()
    T["Gs"] = nc.dram_tensor("Gs", [GN, GN], BF16, kind="ExternalInput").ap()
    T["csc"] = nc.dram_tensor("csc", [128, 256], F32, kind="ExternalInput").ap()
    T["sgn"] = nc.dram_tensor("sgn", [128, 512], F32, kind="ExternalInput").ap()
    for n in ("Gc", "Gs", "csc", "sgn"):
        B[n] = Buf(n)


def group_norm_store(c, T, B, y4, b_y4, gcols, row0, t0, tmp, w=512):
    P = c.P
    sq, bsq = tmp["sq"].next()
    rs, brs = tmp["rs"].next()
    P.op("scalar", lambda e: e.activation(out=sq[:, :, :w], in_=y4[:, :, :w], func=AF.Square), reads=[b_y4], writes=[bsq])
    ps, bps = c.psum.next()

    def mm(e):
        for g in range(4):
            ins = e.matmul(ps[:, :w], c.ones_b[:], sq[:, g, :w], start=(g == 0), stop=(g == 3))
        return ins
    P.op("tensor", mm, reads=[bsq, c.b_const], writes=[bps])
    P.op("scalar", lambda e: e.activation(out=rs[:, :w], in_=ps[:, :w], func=AF.Sqrt, bias=c.eps_col[:, 0:1], scale=1.0 / 512),
         reads=[bps, c.b_const], writes=[brs])
    P.op("vector", lambda e: e.reciprocal(out=rs[:, :w], in_=rs[:, :w]), reads=[brs], writes=[brs])
    for g in range(4):
        st, bst = c.stage.next()
        sv = st[:].bitcast(BF16)[:, :w]
        P.op("vector", lambda e, g=g, sv=sv: e.scalar_tensor_tensor(out=sv, in0=y4[:, g, :w], scalar=gcols[:, g:g + 1], in1=rs[:, :w],
                                                                  op0=ALU.mult, op1=ALU.mult),
             reads=[b_y4, brs, c.b_const], writes=[bst])
        r = row0 + g * 128
        P.dma("sync", T["mixT"][r:r + 128, t0:t0 + w], sv, reads=[bst], mwrites=[B["mixT"]])


def gn_tmp(nc, stack, w=512):
    sq = [stack.enter_context(sbt(nc, "gsq", [128, 4, w], BF16)) for _ in range(2)]
    rs = [stack.enter_context(sbt(nc, "grs", [128, w], F32)) for _ in range(2)]
    return {"sq": Pool([(t, Buf()) for t in sq]), "rs": Pool([(t, Buf()) for t in rs])}


from contextlib import ExitStack


def phase_shortconv(c, T, B, l):
    nc, P = c.nc, c.P
    us = T["usT"]
    with ExitStack() as stack:
        tmp = gn_tmp(nc, stack)
        cts = Pool([(stack.enter_context(sbt(nc, "sc_c", [128, 514], F32)), Buf()) for _ in range(3)])
        hts = Pool([(stack.enter_context(sbt(nc, "sc_h", [128, 514], F32)), Buf()) for _ in range(3)])
        bts = Pool([(stack.enter_context(sbt(nc, "sc_b", [128, 512], F32)), Buf()) for _ in range(3)])
        y4s = Pool([(stack.enter_context(sbt(nc, "sc_y", [128, 4, 512], F32)), Buf()) for _ in range(2)])
        for tt in range(L // 512):
            t0 = tt * 512
            lo = max(t0 - 1, 0)
            hi = min(t0 + 513, L)
            a = lo - (t0 - 1)
            n = hi - lo
            y4, by4 = y4s.next()
            for cc in range(4):
                ct, bc = cts.next()
                ht, bh = hts.next()
                bt, bb = bts.next()
                wcol = c.pp[:, l, PP_SCW + cc * 3:PP_SCW + cc * 3 + 3]
                if a > 0 or n < 514:
                    P.op("vector", lambda e, ct=ct: e.memset(ct[:], 0.0), writes=[bc])
                P.dma("sync", ct[:, a:a + n], us[512 + cc * 128:512 + (cc + 1) * 128, lo:hi], reads=[B["usT"]], writes=[bc])
                P.dma("sync", ht[:, a:a + n], us[1024 + cc * 128:1024 + (cc + 1) * 128, lo:hi], reads=[B["usT"]], writes=[bh])
                P.dma("sync", bt[:], us[cc * 128:(cc + 1) * 128, t0:t0 + 512], reads=[B["usT"]], writes=[bb])
                P.op("vector", lambda e, ct=ct, ht=ht, a=a, n=n: e.tensor_tensor(out=ct[:, a:a + n], in0=ct[:, a:a + n], in1=ht[:, a:a + n], op=ALU.mult),
                     reads=[bc, bh], writes=[bc])
                yv = y4[:, cc, :]
                P.op("vector", lambda e, ct=ct, yv=yv, wcol=wcol: e.tensor_scalar_mul(out=yv, in0=ct[:, 1:513], scalar1=wcol[:, 1:2]),
                     reads=[bc, c.b_const], writes=[by4])
                P.op("vector", lambda e, ct=ct, yv=yv, wcol=wcol: e.scalar_tensor_tensor(out=yv, in0=ct[:, 0:512], scalar=wcol[:, 0:1], in1=yv,
                                                                                      op0=ALU.mult, op1=ALU.add), reads=[bc, c.b_const, by4], writes=[by4])
                P.op("vector", lambda e, ct=ct, yv=yv, wcol=wcol: e.scalar_tensor_tensor(out=yv, in0=ct[:, 2:514], scalar=wcol[:, 2:3], in1=yv,
                                                                                      op0=ALU.mult, op1=ALU.add), reads=[bc, c.b_const, by4], writes=[by4])
                P.op("vector", lambda e, yv=yv, bt=bt: e.tensor_tensor(out=yv, in0=yv, in1=bt[:], op=ALU.mult), reads=[bb, by4], writes=[by4])
            group_norm_store(c, T, B, y4, by4, c.pp[:, l, PP_GRPG + 8:PP_GRPG + 12], 1536, t0, tmp)
    P.barrier()


def phase_fnet(c, T, B, l):
    nc, P = c.nc, c.P
    Gce = T["Gc"].rearrange("(a two) b -> a two b", two=2)
    Gse = T["Gs"].rearrange("(a two) b -> a two b", two=2)
    with ExitStack() as stack:
        AB = stack.enter_context(sbt(nc, "fn_AB", [128, 4, 32, 256], BF16))
        b_AB = Buf("AB")
        csc32 = stack.enter_context(sbt(nc, "fn_csc32", [128, 256], F32))
        csc = stack.enter_context(sbt(nc, "fn_csc", [128, 256], BF16))
        sgn = stack.enter_context(sbt(nc, "fn_sgn", [128, 512], F32))
        b_k = Buf("fnconst")
        P.dma("sync", csc32[:], T["csc"], writes=[b_k])
        P.op("vector", lambda e: e.tensor_copy(out=csc[:], in_=csc32[:]), reads=[b_k], writes=[b_k])
        b_sg = Buf("sgn")
        P.dma("sync", sgn[:], T["sgn"], writes=[b_sg])
        with sbt(nc, "fn_u", [128, 4, L], BF16) as u:
            b_u = Buf("u")
            for g in range(4):
                P.dma("sync", u[:, g, :], T["ufT"][g * 128:(g + 1) * 128, :], reads=[B["ufT"]], mwrites=[b_u])
            for g in range(4):
                for pc in range(32):
                    ps, bps = c.psum.next()
                    P.op("tensor", lambda e, ps=ps, g=g, pc=pc: e.matmul(ps[:, :256], u[:, g, pc * 128:(pc + 1) * 128], csc[:],
                                                                      start=True, stop=True), reads=[b_u, b_k], writes=[bps])
                    eng = c.evac_eng()
                    if eng == "scalar":
                        P.op("scalar", lambda e, ps=ps, g=g, pc=pc: e.copy(out=AB[:, g, pc, :], in_=ps[:, :256]), reads=[bps], mwrites=[b_AB])
                    else:
                        P.op("vector", lambda e, ps=ps, g=g, pc=pc: e.tensor_copy(out=AB[:, g, pc, :], in_=ps[:, :256]), reads=[bps], mwrites=[b_AB])
            P.barrier()
        tmp = gn_tmp(nc, stack)
        FW = 256
        gts = Pool([(stack.enter_context(sbt(nc, "fn_g", [128, 16, FW], BF16)), Buf()) for _ in range(4)])
        y4s = Pool([(stack.enter_context(sbt(nc, "fn_y", [128, 4, FW], F32)), Buf()) for _ in range(2)])
        vts = Pool([(stack.enter_context(sbt(nc, "fn_v", [128, FW], F32)), Buf()) for _ in range(2)])
        for pt in range(L // FW):
            t0 = pt * FW
            gc, bgc = gts.next()
            gs, bgs = gts.next()
            P.dma("sync", gc[:], Gce[0:2048, 0, t0:t0 + FW].rearrange("(k p) b -> p k b", p=128), reads=[B["Gc"]], writes=[bgc])
            P.dma("sync", gs[:], Gse[0:2048, 0, t0:t0 + FW].rearrange("(k p) b -> p k b", p=128), reads=[B["Gs"]], writes=[bgs])
            y4, by4 = y4s.next()
            for g in range(4):
                psu, bpu = c.psum.next()
                psv, bpv = c.psum.next()

                def mm(e, ps, half, g=g, gc=gc, gs=gs):
                    for sc in range(16):
                        e.matmul(ps[:, :FW], AB[:, g, half * 16 + sc, 0:128], gc[:, sc, :], start=(sc == 0), stop=False)
                        ins = e.matmul(ps[:, :FW], AB[:, g, half * 16 + sc, 128:256], gs[:, sc, :], start=False, stop=(sc == 15))
                    return ins
                P.op("tensor", lambda e, mm=mm, psu=psu: mm(e, psu, 0), reads=[b_AB, bgc, bgs], writes=[bpu])
                P.op("tensor", lambda e, mm=mm, psv=psv: mm(e, psv, 1), reads=[b_AB, bgc, bgs], writes=[bpv])
                vt, bvt = vts.next()
                P.op("vector", lambda e, vt=vt, psv=psv: e.tensor_tensor(out=vt[:, :FW], in0=psv[:, :FW], in1=sgn[:, :FW], op=ALU.mult),
                     reads=[bpv, b_sg], writes=[bvt])
                P.op("vector", lambda e, vt=vt, psu=psu, y4=y4, g=g: e.tensor_tensor(out=y4[:, g, :FW], in0=psu[:, :FW], in1=vt[:, :FW], op=ALU.add),
                     reads=[bpu, bvt], writes=[by4])
            group_norm_store(c, T, B, y4, by4, c.pp[:, l, PP_GRPG:PP_GRPG + 4], 512, t0, tmp, w=FW)
    P.barrier()


HY_SCR = [("hcT", [1536, L], F32), ("zt0", [L, 512], BF16), ("zt1", [L, 512], BF16), ("zT1", [512, L], F32),
          ("hfs", [2, 2, L, 512], BF16), ("Hs", [2, 2, GN, 512], F32), ("Ys", [2, GN, 512], BF16)]
NFC = GN // 128


def hy_consts():
    t = np.arange(L, dtype=np.float32) / np.float32(L)
    f = np.linspace(1e-4, 15, 16, dtype=np.float32)
    ang = (np.float32(2.0 * math.pi) * t)[:, None] * f[None, :]
    feats = np.concatenate([t[:, None], np.cos(ang), -np.sin(ang)], axis=-1).astype(np.float32)
    negt = (-(t.astype(np.float64))).astype(np.float32).reshape(32, 128).T
    wf = np.zeros(GN, np.float64)
    wf[:4097] = 2.0 / NF
    wf[0] = 1.0 / NF
    wf[4096] = 1.0 / NF
    wfc = wf.astype(np.float32).reshape(NFC, 128).T
    hk = np.zeros((128, 100), np.float32)
    hk[:, 0:32] = negt
    hk[:, 32:32 + NFC] = wfc
    hk[:, 65:65 + NFC] = -wfc
    return {"featsT": np.ascontiguousarray(feats.T), "hk": hk, "ident": np.eye(128, dtype=np.float32)}


def declare_hy(nc, T, B, dbg_out=()):
    T["featsT"] = nc.dram_tensor("featsT", [33, L], F32, kind="ExternalInput").ap()
    T["hk"] = nc.dram_tensor("hk", [128, 100], F32, kind="ExternalInput").ap()
    T["ident"] = nc.dram_tensor("ident", [128, 128], F32, kind="ExternalInput").ap()
    for n, shp, dt in HY_SCR:
        T[n] = nc.dram_tensor(n, shp, dt, kind="ExternalOutput" if n in dbg_out else "Internal").ap()
    for n in ["featsT", "hk", "ident"] + [x[0] for x in HY_SCR]:
        B[n] = Buf(n)


def setup_ident(c, T, B):
    nc, P = c.nc, c.P
    c.ident = nc.alloc_sbuf_tensor("ident_b", [128, 128], BF16)
    c.hk = nc.alloc_sbuf_tensor("hk_sb", [128, 100], F32)
    c.mpi = nc.alloc_sbuf_tensor("mpi_col", [128, 1], F32)
    P.dma("gpsimd", c.ident[:], T["ident"], mwrites=[c.b_const])
    P.dma("sync", c.hk[:], T["hk"], mwrites=[c.b_const])
    P.op("vector", lambda e: e.memset(c.mpi[:], -math.pi), mwrites=[c.b_const])


def tok_major_store(c, B, src, b_src, dst, bdst, t0):
    P = c.P
    for tl in range(4):
        ps, bps = c.psum.next()
        pb = ps[:].bitcast(BF16)

        def tr(e, pb=pb, tl=tl):
            for cc in range(4):
                ins = e.transpose(out=pb[:, cc * 128:(cc + 1) * 128], in_=src[:, cc, tl * 128:(tl + 1) * 128], identity=c.ident[:])
            return ins
        P.op("tensor", tr, reads=[b_src, c.b_const], writes=[bps])
        st, bst = c.stage.next()
        sv = st[:].bitcast(BF16)[:, :512]
        copy_op(c, c.evac_eng(), sv, pb[:, :512], [bps], [bst])
        r = t0 + tl * 128
        P.dma("sync", dst[r:r + 128, :], sv, reads=[bst], mwrites=[bdst])


def hy_filter_mlp(c, T, B, l):
    nc, P = c.nc, c.P
    with ExitStack() as st_:
        E = st_.enter_context
        fe = E(sbt(nc, "hy_fe", [33, L], F32))
        h1 = E(sbt(nc, "hy_h1", [64, L], F32))
        h2 = E(sbt(nc, "hy_h2", [64, L], F32))
        w1 = E(sbt(nc, "hy_w1", [33, 64], F32))
        w2 = E(sbt(nc, "hy_w2", [64, 64], F32))
        w3 = E(sbt(nc, "hy_w3", [64, 2048], F32))
        fb = E(sbt(nc, "hy_fb", [64, 2], F32))
        dec = E(sbt(nc, "hy_dec", [128, 2048], F32))
        arg = [(E(sbt(nc, "hy_arg", [64, 512], F32)), Buf()) for _ in range(6)]
        argp = Pool(arg)
        wins = Pool([(E(sbt(nc, "hy_win", [128, 2048], F32)), Buf()) for _ in range(2)])
        hws = Pool([(E(sbt(nc, "hy_hw", [128, 2048], F32)), Buf()) for _ in range(2)])
        outs = Pool([(E(sbt(nc, "hy_o", [128, 2, 2, 512], BF16)), Buf()) for _ in range(2)])
        bk = Buf("hyk")
        P.dma("sync", fe[:], T["featsT"], mwrites=[bk])
        P.dma("sync", w1[:], T["hw1"][l], mwrites=[bk])
        P.dma("sync", w2[:], T["hw2"][l], mwrites=[bk])
        P.dma("sync", w3[:], T["hw3"][l], mwrites=[bk])
        P.dma("sync", dec[:], T["hdec"][l].partition_broadcast(128), mwrites=[bk])
        bk2 = Buf("hyk2")
        P.op("scalar", lambda e: e.activation(out=dec[:], in_=dec[:], func=AF.Abs), reads=[bk], writes=[bk2])
        ppl = c.pp[:, l, :]
        P.op("vector", lambda e: e.tensor_tensor(out=fb[:, 0:1], in0=ppl[:64, PP_HB1:PP_HB1 + 1], in1=ppl[:64, PP_HFREQ:PP_HFREQ + 1], op=ALU.mult),
             reads=[c.b_const], writes=[bk2])
        P.op("vector", lambda e: e.tensor_tensor(out=fb[:, 1:2], in0=ppl[:64, PP_HB2:PP_HB2 + 1], in1=ppl[:64, PP_HFREQ:PP_HFREQ + 1], op=ALU.mult),
             reads=[c.b_const, bk2], writes=[bk2])
        b_h1, b_h2 = Buf("h1"), Buf("h2")

        def sin_layer(wt, kin, src, b_srcs, dst, b_dst, j):
            for tt in range(L // 512):
                ts = slice(tt * 512, (tt + 1) * 512)
                ps, bps = c.psum.next()
                P.op("tensor", lambda e, ps=ps, ts=ts: e.matmul(ps[:64, :], wt[:kin, :], src[:kin, ts], start=True, stop=True),
                     reads=b_srcs, writes=[bps])
                a, ba = argp.next()
                P.op("vector", lambda e, ps=ps, a=a: e.tensor_scalar(out=a[:], in0=ps[:64, :], scalar1=ppl[:64, PP_HFREQ:PP_HFREQ + 1],
                                                                  scalar2=fb[:, j:j + 1], op0=ALU.mult, op1=ALU.add),
                     reads=[bps, bk2, c.b_const], writes=[ba])
                m1, bm1 = argp.next()
                m2, bm2 = argp.next()
                P.op("vector", lambda e, a=a, m1=m1: e.tensor_scalar(out=m1[:], in0=a[:], scalar1=math.pi, scalar2=-2.0 * math.pi,
                                                                  op0=ALU.is_gt, op1=ALU.mult), reads=[ba], writes=[bm1])
                P.op("vector", lambda e, a=a, m2=m2: e.tensor_scalar(out=m2[:], in0=a[:], scalar1=-math.pi, scalar2=2.0 * math.pi,
                                                                  op0=ALU.is_lt, op1=ALU.mult), reads=[ba], writes=[bm2])
                P.op("vector", lambda e, m1=m1, m2=m2: e.tensor_tensor(out=m1[:], in0=m1[:], in1=m2[:], op=ALU.add), reads=[bm1, bm2], writes=[bm1])
                P.op("vector", lambda e, a=a, m1=m1: e.tensor_tensor(out=a[:], in0=a[:], in1=m1[:], op=ALU.add), reads=[ba, bm1], writes=[ba])
                P.op("scalar", lambda e, a=a, ts=ts: e.activation(out=dst[:, ts], in_=a[:], func=AF.Sin),
                     reads=[ba, c.b_const], mwrites=[b_dst])
        sin_layer(w1, 33, fe, [bk], h1, b_h1, 0)
        sin_layer(w2, 64, h1, [bk, b_h1], h2, b_h2, 1)
        for tc in range(32):
            win, bw = wins.next()
            hw, bhw = hws.next()
            P.op("scalar", lambda e, win=win, tc=tc: e.activation(out=win[:], in_=dec[:], func=AF.Exp, scale=c.hk[:, tc:tc + 1]),
                 reads=[bk2, c.b_const], writes=[bw])
            for n4 in range(4):
                ps, bps = c.psum.next()
                P.op("tensor", lambda e, ps=ps, tc=tc, n4=n4: e.matmul(ps[:, :], h2[:, tc * 128:(tc + 1) * 128], w3[:, n4 * 512:(n4 + 1) * 512],
                                                                    start=True, stop=True), reads=[b_h2, bk], writes=[bps])
                P.op("vector", lambda e, ps=ps, hw=hw, win=win, n4=n4: e.tensor_tensor(
                    out=hw[:, n4 * 512:(n4 + 1) * 512], in0=ps[:, :], in1=win[:, n4 * 512:(n4 + 1) * 512], op=ALU.mult),
                    reads=[bps, bw], writes=[bhw])
            if tc == 0:
                for o in range(2):
                    P.op("vector", lambda e, hw=hw, o=o: e.memset(hw[0:1, o * 1024 + 512:o * 1024 + 1024], 0.0), reads=[bhw], writes=[bhw])
            ot, bo = outs.next()
            hv = hw[:].rearrange("p (o d c) -> p o d c", o=2, d=2)
            P.op("vector", lambda e, ot=ot, hv=hv: e.tensor_tensor(out=ot[:, :, 0, :], in0=hv[:, :, 0, :], in1=hv[:, :, 1, :], op=ALU.add),
                 reads=[bhw], writes=[bo])
            P.op("gpsimd", lambda e, ot=ot, hv=hv: e.tensor_tensor(out=ot[:, :, 1, :], in0=hv[:, :, 0, :], in1=hv[:, :, 1, :], op=ALU.subtract),
                 reads=[bhw, bo], writes=[bo])
            for o in range(2):
                for sd in range(2):
                    P.dma("sync", T["hfs"][o, sd, tc * 128:(tc + 1) * 128, :], ot[:, o, sd, :], reads=[bo], mwrites=[B["hfs"]])
    P.barrier()


def hy_fwd_dft(c, T, B, z_src, b_z, parts, on_chunk):
    nc, P = c.nc, c.P
    with ExitStack() as st_:
        E = st_.enter_context
        z = E(sbt(nc, "hy_z", [128, 32, 512], BF16))
        gt = Pool([(E(sbt(nc, "hy_gt", [128, 32, 256], BF16)), Buf()) for _ in range(4)])
        bz = Buf("z")
        for q in range(4):
            P.dma("sync", z[:, q * 8:(q + 1) * 8, :], z_src[q * 1024:(q + 1) * 1024, :].rearrange("(k p) c -> p k c", p=128),
                  reads=[b_z], mwrites=[bz])
        Gv = {"c": T["Gc"][0:L, :].rearrange("(k p) f -> p k f", p=128), "s": T["Gs"][0:L, :].rearrange("(k p) f -> p k f", p=128)}
        Gb = {"c": B["Gc"], "s": B["Gs"]}
        for f0 in range(0, NFC, 2):
            nf = min(2, NFC - f0)
            g = {}
            for p_ in parts:
                gtile, bg = gt.next()
                P.dma("sync", gtile[:, :, :nf * 128], Gv[p_][:, :, f0 * 128:(f0 + nf) * 128], reads=[Gb[p_]], writes=[bg])
                g[p_] = (gtile, bg)
            for j in range(nf):
                res = {}
                for p_ in parts:
                    ps, bps = c.psum.next()
                    gtile, bg = g[p_]

                    def mm(e, ps=ps, gtile=gtile, j=j):
                        for tc in range(32):
                            ins = e.matmul(ps[:, :], gtile[:, tc, j * 128:(j + 1) * 128], z[:, tc, :], start=(tc == 0), stop=(tc == 31))
                        return ins
                    P.op("tensor", mm, reads=[bg, bz], writes=[bps])
                    res[p_] = (ps, bps)
                on_chunk(f0 + j, res)
    P.barrier()


def hy_spectrum(c, T, B, o):
    P = c.P
    for which, part, cb in ((0, "c", 32), (1, "s", 65)):
        def on_chunk(fc, res, which=which, part=part, cb=cb):
            ps, bps = res[part]
            col = cb + fc
            st, bst = c.stage.next()
            if fc % 2 == 0:
                P.op("vector", lambda e: e.tensor_scalar_mul(out=st[:], in0=ps[:, :], scalar1=c.hk[:, col:col + 1]),
                     reads=[bps, c.b_const], writes=[bst])
            else:
                P.op("scalar", lambda e: e.mul(out=st[:], in_=ps[:, :], mul=c.hk[:, col:col + 1]),
                     reads=[bps, c.b_const], writes=[bst])
            P.dma("sync", T["Hs"][o, which, fc * 128:(fc + 1) * 128, :], st[:], reads=[bst], mwrites=[B["Hs"]])
        hy_fwd_dft(c, T, B, T["hfs"][o, which], B["hfs"], (part,), on_chunk)


def hy_conv(c, T, B, l, o, zsrc, make_epilogue):
    nc, P = c.nc, c.P
    with ExitStack() as st_:
        E = st_.enter_context
        hts = Pool([(E(sbt(nc, "hy_ht", [128, 2, 512], F32)), Buf()) for _ in range(2)])
        tms = Pool([(E(sbt(nc, "hy_tm", [128, 512], F32)), Buf()) for _ in range(4)])
        ybs = Pool([(E(sbt(nc, "hy_yb", [128, 2, 512], BF16)), Buf()) for _ in range(2)])

        def on_chunk(fc, res):
            psc, bpc = res["c"]
            pss, bpss = res["s"]
            ht, bh = hts.next()
            P.dma("sync", ht[:], T["Hs"][o, :, fc * 128:(fc + 1) * 128, :].rearrange("w p c -> p w c"), reads=[B["Hs"]], writes=[bh])
            t1, b1 = tms.next()
            t2, b2 = tms.next()
            t3, b3 = tms.next()
            t4, b4 = tms.next()
            yb, byb = ybs.next()
            P.op("vector", lambda e: e.tensor_tensor(out=t1[:], in0=psc[:, :], in1=ht[:, 0, :], op=ALU.mult), reads=[bpc, bh], writes=[b1])
            P.op("vector", lambda e: e.tensor_tensor(out=t2[:], in0=pss[:, :], in1=ht[:, 1, :], op=ALU.mult), reads=[bpss, bh], writes=[b2])
            P.op("vector", lambda e: e.tensor_tensor(out=t3[:], in0=pss[:, :], in1=ht[:, 0, :], op=ALU.mult), reads=[bpss, bh], writes=[b3])
            P.op("vector", lambda e: e.tensor_tensor(out=t4[:], in0=psc[:, :], in1=ht[:, 1, :], op=ALU.mult), reads=[bpc, bh], writes=[b4])
            P.op("gpsimd", lambda e: e.tensor_tensor(out=yb[:, 0, :], in0=t1[:], in1=t2[:], op=ALU.add), reads=[b1, b2], writes=[byb])
            P.op("gpsimd", lambda e: e.tensor_tensor(out=yb[:, 1, :], in0=t3[:], in1=t4[:], op=ALU.subtract), reads=[b3, b4, byb], writes=[byb])
            P.dma("sync", T["Ys"][:, fc * 128:(fc + 1) * 128, :].rearrange("w p c -> p w c"), yb[:], reads=[byb], mwrites=[B["Ys"]])
        hy_fwd_dft(c, T, B, zsrc[0], zsrc[1], ("c", "s"), on_chunk)
    FG = 8
    with ExitStack() as st_:
        E = st_.enter_context
        epilogue = make_epilogue(st_)
        yts = Pool([(E(sbt(nc, "hy_yt", [128, 2, FG, 512], BF16)), Buf()) for _ in range(2)])
        gts = Pool([(E(sbt(nc, "hy_gi", [128, 2, FG, 512], BF16)), Buf()) for _ in range(2)])
        Ysv = T["Ys"].rearrange("w (k p) c -> p w k c", p=128)
        Gcv = T["Gc"].rearrange("(k p) t -> p k t", p=128)
        Gsv = T["Gs"].rearrange("(k p) t -> p k t", p=128)
        for tt in range(L // 512):
            pss = [c.psum.next() for _ in range(4)]
            for g0 in range(0, NFC, FG):
                ng = min(FG, NFC - g0)
                yt, byt = yts.next()
                gt, bgt = gts.next()
                for w_ in range(2):
                    P.dma("sync", yt[:, w_, :ng, :], Ysv[:, w_, g0:g0 + ng, :], reads=[B["Ys"]], mwrites=[byt])
                P.dma("sync", gt[:, 0, :ng, :], Gcv[:, g0:g0 + ng, tt * 512:(tt + 1) * 512], reads=[B["Gc"]], mwrites=[bgt])
                P.dma("sync", gt[:, 1, :ng, :], Gsv[:, g0:g0 + ng, tt * 512:(tt + 1) * 512], reads=[B["Gs"]], mwrites=[bgt])

                def mm(e, yt=yt, gt=gt, g0=g0, ng=ng, pss=pss):
                    for cc in range(4):
                        for k in range(ng):
                            for w_ in range(2):
                                ins = e.matmul(pss[cc][0][:, :], yt[:, w_, k, cc * 128:(cc + 1) * 128], gt[:, w_, k, :],
                                               start=(g0 == 0 and k == 0 and w_ == 0), stop=(g0 + k == NFC - 1 and w_ == 1))
                    return ins
                P.op("tensor", mm, reads=[byt, bgt], writes=[p[1] for p in pss])
            for cc in range(4):
                epilogue(tt, cc, pss[cc][0], pss[cc][1])
    P.barrier()


def phase_hyena(c, T, B, l):
    nc, P = c.nc, c.P
    hy_filter_mlp(c, T, B, l)
    for o in range(2):
        hy_spectrum(c, T, B, o)
    uh = T["uhT"]
    with ExitStack() as st_:
        E = st_.enter_context
        uts = Pool([(E(sbt(nc, "hy_ut", [128, 514], F32)), Buf()) for _ in range(3)])
        ots = Pool([(E(sbt(nc, "hy_ot", [128, 512], F32)), Buf()) for _ in range(3)])
        vbs = Pool([(E(sbt(nc, "hy_vb", [128, 4, 512], BF16)), Buf()) for _ in range(2)])
        for tt in range(L // 512):
            t0 = tt * 512
            lo, hi = max(t0 - 1, 0), min(t0 + 513, L)
            a, n = lo - (t0 - 1), hi - lo
            vb, bvb = vbs.next()
            for ch in range(12):
                ut, bu = uts.next()
                ot, bo = ots.next()
                wcol = c.pp[:, l, PP_HCW + ch * 3:PP_HCW + ch * 3 + 3]
                if n < 514:
                    P.op("vector", lambda e, ut=ut: e.memset(ut[:], 0.0), writes=[bu])
                P.dma("sync", ut[:, a:a + n], uh[ch * 128:(ch + 1) * 128, lo:hi], reads=[B["uhT"]], writes=[bu])
                P.op("vector", lambda e, ut=ut, ot=ot, wcol=wcol: e.tensor_scalar_mul(out=ot[:], in0=ut[:, 1:513], scalar1=wcol[:, 1:2]),
                     reads=[bu, c.b_const], writes=[bo])
                P.op("vector", lambda e, ut=ut, ot=ot, wcol=wcol: e.scalar_tensor_tensor(out=ot[:], in0=ut[:, 0:512], scalar=wcol[:, 0:1], in1=ot[:],
                                                                                      op0=ALU.mult, op1=ALU.add), reads=[bu, bo, c.b_const], writes=[bo])
                P.op("vector", lambda e, ut=ut, ot=ot, wcol=wcol: e.scalar_tensor_tensor(out=ot[:], in0=ut[:, 2:514], scalar=wcol[:, 2:3], in1=ot[:],
                                                                                      op0=ALU.mult, op1=ALU.add), reads=[bu, bo, c.b_const], writes=[bo])
                P.dma("sync", T["hcT"][ch * 128:(ch + 1) * 128, t0:t0 + 512], ot[:], reads=[bo], mwrites=[B["hcT"]])
                if ch < 4:
                    P.op("scalar", lambda e, vb=vb, ot=ot, ch=ch: e.copy(out=vb[:, ch, :], in_=ot[:]), reads=[bo], mwrites=[bvb])
            tok_major_store(c, B, vb, bvb, T["zt0"], B["zt0"], t0)
    P.barrier()
    def mk0(st_):
        E = st_.enter_context
        xts = Pool([(E(sbt(nc, "hy_x", [128, 2, 512], F32)), Buf()) for _ in range(3)])
        zbs = Pool([(E(sbt(nc, "hy_zb", [128, 4, 512], BF16)), Buf()) for _ in range(2)])
        cur = {}

        def epi0(tt, cc, ps, bps):
            t0 = tt * 512
            if cc == 0:
                cur["zb"] = zbs.next()
            zb, bzb = cur["zb"]
            xt, bx = xts.next()
            P.dma("sync", xt[:, 0, :], T["hcT"][cc * 128:(cc + 1) * 128, t0:t0 + 512], reads=[B["hcT"]], mwrites=[bx])
            P.dma("sync", xt[:, 1, :], T["hcT"][512 + cc * 128:512 + (cc + 1) * 128, t0:t0 + 512], reads=[B["hcT"]], mwrites=[bx])
            sk = c.pp[:, l, PP_HSKIP + cc:PP_HSKIP + cc + 1]
            P.op("vector", lambda e: e.scalar_tensor_tensor(out=xt[:, 0, :], in0=xt[:, 0, :], scalar=sk, in1=ps[:, :], op0=ALU.mult, op1=ALU.add),
                 reads=[bps, bx, c.b_const], writes=[bx])
            P.op("vector", lambda e: e.tensor_tensor(out=xt[:, 0, :], in0=xt[:, 0, :], in1=xt[:, 1, :], op=ALU.mult), reads=[bx], writes=[bx])
            P.dma("sync", T["zT1"][cc * 128:(cc + 1) * 128, t0:t0 + 512], xt[:, 0, :], reads=[bx], mwrites=[B["zT1"]])
            P.op("scalar", lambda e: e.copy(out=zb[:, cc, :], in_=xt[:, 0, :]), reads=[bx], mwrites=[bzb])
            if cc == 3:
                tok_major_store(c, B, zb, bzb, T["zt1"], B["zt1"], t0)
        return epi0
    hy_conv(c, T, B, l, 0, (T["zt0"], B["zt0"]), mk0)

    def mk1(st_):
        E = st_.enter_context
        tmp = gn_tmp(nc, st_)
        xts = Pool([(E(sbt(nc, "hy_x", [128, 2, 512], F32)), Buf()) for _ in range(3)])
        y4s = Pool([(E(sbt(nc, "hy_y4", [128, 4, 512], F32)), Buf()) for _ in range(2)])
        cur = {}

        def epi1(tt, cc, ps, bps):
            t0 = tt * 512
            if cc == 0:
                cur["y4"] = y4s.next()
            y4, by4 = cur["y4"]
            xt, bx = xts.next()
            P.dma("sync", xt[:, 0, :], T["zT1"][cc * 128:(cc + 1) * 128, t0:t0 + 512], reads=[B["zT1"]], mwrites=[bx])
            P.dma("sync", xt[:, 1, :], T["hcT"][1024 + cc * 128:1024 + (cc + 1) * 128, t0:t0 + 512], reads=[B["hcT"]], mwrites=[bx])
            sk = c.pp[:, l, PP_HSKIP + 4 + cc:PP_HSKIP + 4 + cc + 1]
            P.op("vector", lambda e: e.scalar_tensor_tensor(out=xt[:, 0, :], in0=xt[:, 0, :], scalar=sk, in1=ps[:, :], op0=ALU.mult, op1=ALU.add),
                 reads=[bps, bx, c.b_const], writes=[bx])
            P.op("vector", lambda e: e.tensor_tensor(out=y4[:, cc, :], in0=xt[:, 0, :], in1=xt[:, 1, :], op=ALU.mult), reads=[bx, by4], writes=[by4])
            if cc == 3:
                group_norm_store(c, T, B, y4, by4, c.pp[:, l, PP_GRPG + 4:PP_GRPG + 8], 1024, t0, tmp)
        return epi1
    hy_conv(c, T, B, l, 1, (T["zt1"], B["zt1"]), mk1)


def gla_consts():
    s = np.arange(128)
    Uf = np.where(s[:, None] <= s[None, :], -1.0 / 16.0, 0.0)
    Ub = np.where(s[:, None] >= s[None, :], -1.0 / 16.0, 0.0)
    Mf = np.where(s[:, None] <= s[None, :], 1.0, 0.0)
    Mb = np.where(s[:, None] > s[None, :], 1.0, 0.0)
    g = np.zeros((128, 2, 128 + 512), np.float32)
    g[:, 0, :128] = Uf
    g[:, 1, :128] = Ub
    g[:, 0, 128:] = np.tile(Mf, (1, 4))
    g[:, 1, 128:] = np.tile(Mb, (1, 4))
    return {"glac": g}


def declare_gla(nc, T, B, dbg_out=()):
    T["glac"] = nc.dram_tensor("glac", [128, 2, 640], F32, kind="ExternalInput").ap()
    T["ofT"] = nc.dram_tensor("ofT", [512, L], F32, kind="ExternalOutput" if "ofT" in dbg_out else "Internal").ap()
    B["glac"] = Buf()
    B["ofT"] = Buf()


import os
GLA_LEVEL = int(os.environ.get("GLA_LEVEL", "99"))


def dbg_dump(c, name, ap, buf, shape, dt):
    if not getattr(c, "dbg", False):
        return
    t = c.nc.dram_tensor("dbg_" + name, shape, dt, kind="ExternalOutput").ap()
    c.P.dma("sync", t, ap, reads=[buf], store=True)


def phase_gla(c, T, B, l):
    nc, P = c.nc, c.P
    with ExitStack() as st_:
        E = st_.enter_context
        gc = E(sbt(nc, "gl_c", [128, 2, 640], F32))
        gw = E(sbt(nc, "gl_w", [32, 2, 256], F32))
        gb = E(sbt(nc, "gl_b", [1, 2, 256], F32))
        onesf = E(sbt(nc, "gl_1", [128, 128], F32))
        bk = Buf("glk")
        P.dma("sync", gc[:], T["glac"], mwrites=[bk])
        for d in range(2):
            P.dma("sync", gw[:, d, :], T["gkw"][l, d], mwrites=[bk])
            P.dma("sync", gb[:, d, :], T["gkb"][l, d:d + 1, :], mwrites=[bk])
        P.op("vector", lambda e: e.memset(onesf[:], 1.0), mwrites=[bk])

        def mk(name, shape, dt, n=2):
            return Pool([(E(sbt(nc, name, shape, dt)), Buf()) for _ in range(n)])
        qts, kts = mk("gl_q", [128, 2, 512], F32), mk("gl_k", [128, 2, 512], F32)
        lrs, vts = mk("gl_lr", [32, 512], F32), mk("gl_v", [128, 4, 512], BF16)
        gts = mk("gl_g", [128, 4, 512], F32)
        e1s, ls = mk("gl_e1", [128, 256], F32), mk("gl_l", [128, 256], F32)
        bsbs = mk("gl_bsb", [128, 2, 128], F32)
        cols = mk("gl_col", [128, 8], F32)
        Es = mk("gl_E", [128, 4, 2, 128], F32)
        qk16 = mk("gl_qk", [128, 4, 2, 128], BF16)
        qzs = mk("gl_qz", [128, 2, 2, 2, 128], BF16)
        for _ in range(2):
            qz_, bqz_ = qzs.next()
            P.op("vector", lambda e, qz_=qz_: e.memset(qz_[:], 0.0), writes=[bqz_])
        klTs = mk("gl_klT", [128, 256], BF16)
        sTms = mk("gl_sTm", [128, 512], BF16)
        S32 = E(sbt(nc, "gl_S", [128, 2, 128], F32))
        Sbf = mk("gl_Sbf", [128, 2, 128], BF16)
        ocs, sqs = mk("gl_oc", [128, 512], F32), mk("gl_sq", [128, 512], BF16)
        rss, sgs = mk("gl_rs", [128, 512], F32), mk("gl_sg", [128, 512], F32)
        ofs = mk("gl_of", [128, 512], F32)
        b_S = Buf("S")
        qv = T["qT"].rearrange("(c p) t -> p c t", p=128)
        kv = T["kT"].rearrange("(c p) t -> p c t", p=128)
        gv = T["gT"].rearrange("(h p) t -> p h t", p=128)
        ofv = T["ofT"].rearrange("(h p) t -> p h t", p=128)
        mxv = T["mixT"][0:512, :].rearrange("(h p) t -> p h t", p=128)
        for d in range(2):
            REF, LAST = (64, 127) if d == 0 else (63, 0)
            P.op("vector", lambda e: e.memset(S32[:], 0.0), writes=[b_S])
            sb0, bsb0 = Sbf.next()
            P.op("vector", lambda e, sb0=sb0: e.memset(sb0[:], 0.0), writes=[bsb0])
            cur_sbf = (sb0, bsb0)
            groups = range(8) if d == 0 else range(7, -1, -1)
            for grp in groups:
                t0 = grp * 512
                qt, bq = qts.next()
                kt, bkt = kts.next()
                lr, blr = lrs.next()
                vt, bv = vts.next()
                P.dma("sync", qt[:], qv[:, :, t0:t0 + 512], reads=[B["qT"]], writes=[bq])
                P.dma("sync", kt[:], kv[:, :, t0:t0 + 512], reads=[B["kT"]], writes=[bkt])
                P.dma("sync", lr[:], T["lrT"][:, t0:t0 + 512], reads=[B["lrT"]], writes=[blr])
                P.dma("sync", vt[:], T["vtok"][t0:t0 + 512, :].rearrange("(b p) c -> p b c", p=128), reads=[B["vtok"]], writes=[bv])
                if d == 1:
                    gt, bg = gts.next()
                    P.dma("sync", gt[:], gv[:, :, t0:t0 + 512], reads=[B["gT"]], writes=[bg])
                blocks = range(4) if d == 0 else range(3, -1, -1)
                for blk in blocks:
                    bs = slice(blk * 128, (blk + 1) * 128)
                    tb = t0 + blk * 128
                    ps, bps = c.psum.next()

                    def mm1(e, ps=ps, lr=lr, bs=bs, d=d):
                        e.matmul(ps[:, :256], lr[:, bs], gw[:, d, :], start=True, stop=False)
                        return e.matmul(ps[:, :256], onesf[0:1, :], gb[0:1, d, :], start=False, stop=True)
                    P.op("tensor", mm1, reads=[blr, bk], writes=[bps])
                    e1, be1 = e1s.next()
                    lt, bl = ls.next()
                    P.op("scalar", lambda e, e1=e1, ps=ps: e.activation(out=e1[:], in_=ps[:, :256], func=AF.Exp, scale=-1.0), reads=[bps], writes=[be1])
                    P.op("scalar", lambda e, e1=e1, lt=lt: e.activation(out=lt[:], in_=e1[:], func=AF.Ln, bias=onesf[:, 0:1]), reads=[be1, bk], writes=[bl])
                    ps2, bps2 = c.psum.next()

                    def mm2(e, ps2=ps2, lt=lt, d=d):
                        for cc in range(2):
                            ins = e.matmul(ps2[:, cc * 128:(cc + 1) * 128], lt[:, cc * 128:(cc + 1) * 128], gc[:, d, 0:128], start=True, stop=True)
                        return ins
                    P.op("tensor", mm2, reads=[bl, bk], writes=[bps2])
                    if GLA_LEVEL < 2:
                        continue
                    bsb, bbsb = bsbs.next()
                    P.op("vector", lambda e, bsb=bsb, ps2=ps2: e.tensor_copy(out=bsb[:].rearrange("p c t -> p (c t)"), in_=ps2[:, :256]), reads=[bps2], writes=[bbsb])
                    col, bcol = cols.next()
                    P.op("vector", lambda e, col=col, bsb=bsb, REF=REF: e.tensor_scalar_mul(out=col[:, 0:2], in0=bsb[:, :, REF], scalar1=-1.0), reads=[bbsb], writes=[bcol])
                    P.op("vector", lambda e, col=col, bsb=bsb, REF=REF: e.tensor_copy(out=col[:, 2:4], in_=bsb[:, :, REF]), reads=[bbsb, bcol], writes=[bcol])
                    P.op("vector", lambda e, col=col, bsb=bsb, LAST=LAST: e.tensor_copy(out=col[:, 4:6], in_=bsb[:, :, LAST]), reads=[bbsb, bcol], writes=[bcol])
                    P.op("scalar", lambda e, col=col: e.activation(out=col[:, 6:8], in_=col[:, 4:6], func=AF.Exp), reads=[bcol], writes=[bcol])
                    Et, bE = Es.next()
                    for cc in range(2):
                        P.op("scalar", lambda e, Et=Et, bsb=bsb, col=col, cc=cc: e.activation(out=Et[:, 0, cc, :], in_=bsb[:, cc, :], func=AF.Exp, bias=col[:, cc:cc + 1]),
                             reads=[bbsb, bcol, bE], writes=[bE])
                        P.op("scalar", lambda e, Et=Et, bsb=bsb, col=col, cc=cc: e.activation(out=Et[:, 1, cc, :], in_=bsb[:, cc, :], func=AF.Exp, bias=col[:, 2 + cc:3 + cc], scale=-1.0),
                             reads=[bbsb, bcol, bE], writes=[bE])
                        P.op("scalar", lambda e, Et=Et, bsb=bsb, col=col, cc=cc: e.activation(out=Et[:, 3, cc, :], in_=bsb[:, cc, :], func=AF.Exp, bias=col[:, 4 + cc:5 + cc], scale=-1.0),
                             reads=[bbsb, bcol, bE], writes=[bE])
                    P.op("scalar", lambda e, Et=Et, bsb=bsb: e.activation(out=Et[:, 2, :, :], in_=bsb[:], func=AF.Exp), reads=[bbsb, bE], writes=[bE])
                    if GLA_LEVEL < 3:
                        continue
                    qk, bqk = qk16.next()
                    qz, bqz = qzs.next()
                    for hh in range(2):
                        pr = slice(hh * 64, (hh + 1) * 64)
                        P.op("vector", lambda e, qz=qz, qt=qt, Et=Et, bs=bs, pr=pr, hh=hh: e.scalar_tensor_tensor(
                            out=qz[pr, hh, 0, :, :], in0=qt[pr, :, bs], scalar=0.125, in1=Et[pr, 0, :, :], op0=ALU.mult, op1=ALU.mult),
                            reads=[bq, bE], mwrites=[bqz])
                        P.op("vector", lambda e, qz=qz, qt=qt, Et=Et, bs=bs, pr=pr, hh=hh: e.scalar_tensor_tensor(
                            out=qz[pr, hh, 1, :, :], in0=qt[pr, :, bs], scalar=0.125, in1=Et[pr, 2, :, :], op0=ALU.mult, op1=ALU.mult),
                            reads=[bq, bE], mwrites=[bqz])
                    P.op("vector", lambda e, qk=qk, kt=kt, Et=Et, bs=bs: e.tensor_tensor(out=qk[:, 1, :, :], in0=kt[:, :, bs], in1=Et[:, 1, :, :], op=ALU.mult),
                         reads=[bkt, bE], writes=[bqk])
                    P.op("vector", lambda e, qk=qk, kt=kt, Et=Et, bs=bs: e.tensor_tensor(out=qk[:, 3, :, :], in0=kt[:, :, bs], in1=Et[:, 3, :, :], op=ALU.mult),
                         reads=[bkt, bE, bqk], writes=[bqk])
                    if GLA_LEVEL < 4:
                        continue
                    ps3, bps3 = c.psum.next()
                    pb3 = ps3[:].bitcast(BF16)

                    def tr(e, pb3=pb3, qk=qk):
                        for cc in range(2):
                            ins = e.transpose(out=pb3[:, cc * 128:(cc + 1) * 128], in_=qk[:, 3, cc, :], identity=c.ident[:])
                        return ins
                    P.op("tensor", tr, reads=[bqk, c.b_const], writes=[bps3])
                    klT, bklT = klTs.next()
                    P.op("vector", lambda e, klT=klT, pb3=pb3: e.tensor_copy(out=klT[:], in_=pb3[:, :256]), reads=[bps3], writes=[bklT])
                    ps4, bps4 = c.psum.next()

                    def mm4(e, ps4=ps4, qk=qk, qz=qz):
                        for h in range(4):
                            cc, hh = h // 2, h % 2
                            ins = e.matmul(ps4[:, h * 128:(h + 1) * 128], qk[:, 1, cc, :], qz[:, hh, 0, cc, :], start=True, stop=True)
                        return ins
                    P.op("tensor", mm4, reads=[bqk, bqz], writes=[bps4])
                    sTm, bsT = sTms.next()
                    P.op("vector", lambda e, sTm=sTm, ps4=ps4, d=d: e.tensor_tensor(out=sTm[:], in0=ps4[:, :], in1=gc[:, d, 128:640], op=ALU.mult), reads=[bps4, bk], writes=[bsT])
                    if GLA_LEVEL < 6:
                        continue
                    sbf, bsbf = cur_sbf
                    ps5, bps5 = c.psum.next()

                    def mm5(e, ps5=ps5, vt=vt, sTm=sTm, qz=qz, sbf=sbf, blk=blk):
                        for h in range(4):
                            cc, hh = h // 2, h % 2
                            e.matmul(ps5[:, h * 128:(h + 1) * 128], vt[:, blk, h * 128:(h + 1) * 128], sTm[:, h * 128:(h + 1) * 128], start=True, stop=False)
                            ins = e.matmul(ps5[:, h * 128:(h + 1) * 128], sbf[:, cc, :], qz[:, hh, 1, cc, :], start=False, stop=True)
                        return ins
                    P.op("tensor", mm5, reads=[bv, bsT, bqz, bsbf], writes=[bps5])
                    if GLA_LEVEL < 7:
                        continue
                    ps6, bps6 = c.psum.next()

                    def mm6(e, ps6=ps6, klT=klT, vt=vt, blk=blk):
                        for cc in range(2):
                            ins = e.matmul(ps6[:, cc * 256:(cc + 1) * 256], klT[:, cc * 128:(cc + 1) * 128], vt[:, blk, cc * 256:(cc + 1) * 256], start=True, stop=True)
                        return ins
                    P.op("tensor", mm6, reads=[bklT, bv], writes=[bps6])
                    for h in range(4):
                        cc, hh = h // 2, h % 2
                        pr = slice(hh * 64, (hh + 1) * 64)
                        P.op("vector", lambda e, cc=cc, hh=hh, pr=pr, col=col, ps6=ps6: e.scalar_tensor_tensor(
                            out=S32[pr, cc, :], in0=S32[pr, cc, :], scalar=col[pr, 6 + cc:7 + cc], in1=ps6[pr, cc * 256 + hh * 128:cc * 256 + (hh + 1) * 128],
                            op0=ALU.mult, op1=ALU.add), reads=[b_S, bcol, bps6], writes=[b_S])
                    nsb, bnsb = Sbf.next()
                    P.op("vector", lambda e, nsb=nsb: e.tensor_copy(out=nsb[:], in_=S32[:]), reads=[b_S], writes=[bnsb])
                    cur_sbf = (nsb, bnsb)
                    if GLA_LEVEL < 8:
                        continue
                    if d == 0 and grp == 0 and blk == 1:
                        dbg_dump(c, "lt", lt[:], bl, [128, 256], F32)
                        dbg_dump(c, "bsb", bsb[:], bbsb, [128, 2, 128], F32)
                        dbg_dump(c, "col", col[:], bcol, [128, 8], F32)
                        dbg_dump(c, "Et", Et[:], bE, [128, 4, 2, 128], F32)
                        dbg_dump(c, "qk", qk[:], bqk, [128, 4, 2, 128], BF16)
                        dbg_dump(c, "qz", qz[:], bqz, [128, 2, 2, 2, 128], BF16)
                        dbg_dump(c, "klT", klT[:], bklT, [128, 256], BF16)
                        dbg_dump(c, "sTm", sTm[:], bsT, [128, 512], BF16)
                        dbg_dump(c, "S32", S32[:], b_S, [128, 2, 128], F32)
                    if d == 0:
                        oc, boc = ocs.next()
                        P.op("scalar", lambda e, oc=oc, ps5=ps5: e.copy(out=oc[:], in_=ps5[:, :]), reads=[bps5], writes=[boc])
                        P.dma("sync", ofv[:, :, tb:tb + 128], oc[:].rearrange("p (h t) -> p h t", h=4), reads=[boc], mwrites=[B["ofT"]])
                    else:
                        of, bof = ofs.next()
                        P.dma("sync", of[:].rearrange("p (h t) -> p h t", h=4), ofv[:, :, tb:tb + 128], reads=[B["ofT"]], writes=[bof])
                        oc, boc = ocs.next()
                        P.op("vector", lambda e, oc=oc, ps5=ps5, of=of: e.tensor_tensor(out=oc[:], in0=ps5[:, :], in1=of[:], op=ALU.add), reads=[bps5, bof], writes=[boc])
                        sq, bsq = sqs.next()
                        P.op("scalar", lambda e, sq=sq, oc=oc: e.activation(out=sq[:], in_=oc[:], func=AF.Square), reads=[boc], writes=[bsq])
                        ps7, bps7 = c.psum.next()
                        P.op("tensor", lambda e, ps7=ps7, sq=sq: e.matmul(ps7[:, :], c.ones_b[:], sq[:], start=True, stop=True), reads=[bsq, c.b_const], writes=[bps7])
                        rs, brs = rss.next()
                        P.op("scalar", lambda e, rs=rs, ps7=ps7: e.activation(out=rs[:], in_=ps7[:, :], func=AF.Sqrt, bias=c.eps_col[:, 0:1], scale=1.0 / 128),
                             reads=[bps7, c.b_const], writes=[brs])
                        P.op("vector", lambda e, rs=rs: e.reciprocal(out=rs[:], in_=rs[:]), reads=[brs], writes=[brs])
                        sg, bsg = sgs.next()
                        P.op("scalar", lambda e, sg=sg, gt=gt, bs=bs: e.activation(out=sg[:].rearrange("p (h t) -> p h t", h=4), in_=gt[:, :, bs], func=AF.Silu),
                             reads=[bg], writes=[bsg])
                        P.op("vector", lambda e, oc=oc, rs=rs: e.tensor_tensor(out=oc[:], in0=oc[:], in1=rs[:], op=ALU.mult), reads=[boc, brs], writes=[boc])
                        st, bst = c.stage.next()
                        sv = st[:].bitcast(BF16)[:, :512]
                        P.op("vector", lambda e, sv=sv, oc=oc, sg=sg: e.scalar_tensor_tensor(out=sv, in0=oc[:], scalar=c.pp[:, l, PP_GLAG:PP_GLAG + 1], in1=sg[:],
                                                                                          op0=ALU.mult, op1=ALU.mult), reads=[boc, bsg, c.b_const], writes=[bst])
                        P.dma("sync", mxv[:, :, tb:tb + 128], sv.rearrange("p (h t) -> p h t", h=4), reads=[bst], mwrites=[B["mixT"]])
    P.barrier()


def build_program(layers=(0, 1, 2, 3), final=True, dbg_out=()):
    nc = bass.Bass("TRN2", target_bir_lowering=False)
    T, B = declare(nc, dbg_out=dbg_out)
    declare_consts(nc, T, B)
    declare_hy(nc, T, B)
    declare_gla(nc, T, B)
    T["xresB"] = nc.dram_tensor("xresB", [D, L], F32, kind="Internal").ap()
    B["xresB"] = Buf("xresB")
    T["yT"] = nc.dram_tensor("yT", [D, L], F32, kind="ExternalOutput").ap()
    B["yT"] = Buf("yT")
    c = Ctx(nc)
    load_params(c, T, B)
    setup_ident(c, T, B)
    setup_mem(c, T, B)
    cur = "xT"
    for l in layers:
        phase_inproj(c, T, B, l, cur)
        c.P.store_q = STORE_Q
        phase_gla(c, T, B, l)
        phase_fnet(c, T, B, l)
        phase_hyena(c, T, B, l)
        phase_shortconv(c, T, B, l)
        c.P.store_q = None
        a, b = ("xres", "xresB") if cur != "xres" else ("xresB", "xres")
        phase_outproj(c, T, B, l, cur, a)
        phase_xattn(c, T, B, l, a, b)
        phase_ffn(c, T, B, l, b, a)
        cur = a
    if final:
        rmsnorm_fm(c, T[cur], B[cur], 0, L, c.pg[:, 16:32], None, None, out_dram=(T["yT"], B["yT"]))
    else:
        for k in range(KC):
            st, bst = c.stage.next()
            for tt in range(L // 512):
                st, bst = c.stage.next()
                c.P.dma("sync", st[:], T[cur][k * 128:(k + 1) * 128, tt * 512:(tt + 1) * 512], reads=[B[cur]], writes=[bst])
                c.P.dma("sync", T["yT"][k * 128:(k + 1) * 128, tt * 512:(tt + 1) * 512], st[:], reads=[bst], mwrites=[B["yT"]], store=True)
    c.P.finish()
    return nc, c


def pack_small(inp):
    g = lambda k: np.asarray(inp[k], dtype=np.float32)
    pp = np.zeros((DEPTH, 128, NPP), np.float32)
    for l in range(DEPTH):
        for j in range(3):
            pp[l, :, PP_NG + 16 * j: PP_NG + 16 * (j + 1)] = g("norm_g")[l, j].reshape(16, 128).T
        pp[l, :, PP_GLAG] = g("gla_norm_g")[l]
        pp[l, :, PP_HCW:PP_HCW + 36] = g("hy_conv_w")[l].reshape(3, 12, 128).transpose(2, 1, 0).reshape(128, 36)
        pp[l, :, PP_HSKIP:PP_HSKIP + 8] = g("hy_skip")[l].reshape(2, 4, 128).transpose(2, 0, 1).reshape(128, 8)
        pp[l, :, PP_SCW:PP_SCW + 12] = g("sc_conv_w")[l].reshape(3, 4, 128).transpose(2, 1, 0).reshape(128, 12)
        pp[l, :, PP_GRPG:PP_GRPG + 12] = g("grp_norm_g")[l].reshape(3, 4, 128).transpose(2, 0, 1).reshape(128, 12)
        pp[l, :64, PP_HB1] = g("hy_ffn_b1")[l]
        pp[l, :64, PP_HB2] = g("hy_ffn_b2")[l]
        pp[l, :64, PP_HFREQ] = g("hy_sin_freq")[l]
    pg = np.zeros((128, 32), np.float32)
    pg[:, 0:16] = g("mem_norm_g").reshape(16, 128).T
    pg[:, 16:32] = g("final_norm_g").reshape(16, 128).T
    gkw = np.zeros((DEPTH, 2, 32, 256), np.float32)
    gkw[:, 0, :16] = g("gla_gk_w")[:, 0]
    gkw[:, 1, 16:] = g("gla_gk_w")[:, 1]
    return {"pp": pp, "pg": pg, "gkw": gkw, "gkb": np.ascontiguousarray(g("gla_gk_b")),
            "hw1": np.ascontiguousarray(g("hy_ffn_w1")), "hw2": np.ascontiguousarray(g("hy_ffn_w2")),
            "hw3": np.ascontiguousarray(g("hy_ffn_w3")), "hdec": np.ascontiguousarray(g("hy_decay").reshape(DEPTH, 2048))}


_CONSTS = None


def all_consts():
    global _CONSTS
    if _CONSTS is None:
        cst = {}
        cst.update(host_consts())
        cst.update(hy_consts())
        cst.update(gla_consts())
        _CONSTS = cst
    return _CONSTS


def make_in_maps(inputs, ncores=8):
    shared = {}
    for n, _ in WEIGHTS:
        shared[n] = np.ascontiguousarray(np.asarray(inputs[n], dtype=np.float32))
    shared.update(pack_small(inputs))
    shared.update(all_consts())
    x = np.asarray(inputs["x"], dtype=np.float32)
    mem = np.asarray(inputs["mem"], dtype=np.float32)
    maps = []
    for b in range(ncores):
        m = dict(shared)
        m["xT"] = np.ascontiguousarray(x[b].T)
        m["memT"] = np.ascontiguousarray(mem[b].T)
        maps.append(m)
    return maps


def kernel(**inputs):
    nc, c = build_program()
    maps = make_in_maps(inputs, 8)
    res = run_bass_kernel_spmd(nc, maps, core_ids=list(range(8)))
    out = np.empty((8, L, D), np.float32)
    for b in range(8):
        out[b] = np.asarray(res.results[b]["yT"]).T
    return out
```
